# Optimizing a Trainium2 kernel written in Bass

```python
import jax, jax.numpy as jnp
from jax import lax
import numpy as np

D_MODEL = 1024
BATCH = 8
SEQ = 2048
DEPTH = 4

CHUNK = 64
Q_BLOCK = 128
EPS = 1e-6

CONV_WIDTH = 512
CONV_KERNEL = 31

FOX_HEADS = 8
FOX_HEAD_DIM = 64
FOX_WIDTH = FOX_HEADS * FOX_HEAD_DIM

LRU_WIDTH = 512
LRU_BLOCKS = 8
LRU_BLOCK_DIM = LRU_WIDTH // LRU_BLOCKS
LRU_CONV = 4
LRU_C = 8.0

N_BRANCH = 3
FFN_HIDDEN = -(-8 * D_MODEL // (3 * 256)) * 256

IN_SIZES = (2 * CONV_WIDTH, FOX_WIDTH, FOX_WIDTH, FOX_WIDTH, FOX_HEADS,
            LRU_WIDTH, LRU_WIDTH, N_BRANCH * D_MODEL)
IN_COLS = sum(IN_SIZES)

kernel_name = "hybrid_conv_fox_rglru_gated_trunk"


def rms_norm(x, g):
    xf = x.astype(jnp.float32)
    y = xf * lax.rsqrt(jnp.mean(xf * xf, axis=-1, keepdims=True) + EPS)
    return (y * g.astype(jnp.float32)).astype(x.dtype)


def layer_norm(x, g, b):
    xf = x.astype(jnp.float32)
    mu = jnp.mean(xf, axis=-1, keepdims=True)
    var = jnp.mean(jnp.square(xf - mu), axis=-1, keepdims=True)
    y = (xf - mu) * lax.rsqrt(var + EPS)
    return (y * g.astype(jnp.float32) + b.astype(jnp.float32)).astype(x.dtype)


def causal_depthwise_conv(u, w, b):
    k = w.shape[0]
    y = lax.conv_general_dilated(
        u, w[:, None, :].astype(u.dtype), window_strides=(1,), padding=[(k - 1, 0)],
        dimension_numbers=("NWC", "WIO", "NWC"), feature_group_count=u.shape[-1])
    return y + b


def conformer_conv_branch(u_glu, dw_w, dw_b, ln_g, ln_b, w_out):
    a, g = jnp.split(u_glu, 2, axis=-1)
    v = a * jax.nn.sigmoid(g)
    v = causal_depthwise_conv(v, dw_w, dw_b)
    v = jax.nn.silu(layer_norm(v, ln_g, ln_b))
    return v @ w_out


def forgetting_attention_branch(q, k, v, f_logit, b_f, qn_g, kn_g, w_out):
    bsz, seq, _ = q.shape
    shp = (bsz, seq, FOX_HEADS, FOX_HEAD_DIM)
    qh = rms_norm(q.reshape(shp), qn_g).transpose(0, 2, 1, 3)
    kh = rms_norm(k.reshape(shp), kn_g).transpose(0, 2, 1, 3)
    vh = v.reshape(shp).transpose(0, 2, 1, 3)
    log_f = jax.nn.log_sigmoid(f_logit.astype(jnp.float32) + b_f.astype(jnp.float32))
    cum = jnp.cumsum(log_f, axis=1).transpose(0, 2, 1)
    scale = FOX_HEAD_DIM ** -0.5
    outs = []
    for q0 in range(0, seq, Q_BLOCK):
        q1 = q0 + Q_BLOCK
        logits = jnp.einsum("bhqd,bhkd->bhqk", qh[:, :, q0:q1], kh[:, :, :q1]).astype(jnp.float32) * scale
        logits = logits + cum[:, :, q0:q1, None] - cum[:, :, None, :q1]
        causal = (q0 + jnp.arange(Q_BLOCK))[:, None] >= jnp.arange(q1)[None, :]
        logits = jnp.where(causal, logits, -jnp.inf)
        p = jax.nn.softmax(logits, axis=-1).astype(vh.dtype)
        outs.append(jnp.einsum("bhqk,bhkd->bhqd", p, vh[:, :, :q1]))
    o = jnp.concatenate(outs, axis=2).transpose(0, 2, 1, 3).reshape(bsz, seq, FOX_WIDTH)
    return o @ w_out


def rg_lru_branch(y_in, x_in, conv_w, conv_b, w_r, b_r, w_i, b_i, lam, w_out):
    bsz, seq, _ = x_in.shape
    xr = causal_depthwise_conv(x_in, conv_w, conv_b)
    xb = xr.reshape(bsz, seq, LRU_BLOCKS, LRU_BLOCK_DIM)
    r = jax.nn.sigmoid(jnp.einsum("bsnd,nde->bsne", xb, w_r).reshape(bsz, seq, LRU_WIDTH) + b_r)
    i = jax.nn.sigmoid(jnp.einsum("bsnd,nde->bsne", xb, w_i).reshape(bsz, seq, LRU_WIDTH) + b_i)
    log_a = -LRU_C * r.astype(jnp.float32) * jax.nn.softplus(-lam.astype(jnp.float32))
    a = jnp.exp(log_a)
    u = jnp.sqrt(-jnp.expm1(2.0 * log_a)) * (i * xr).astype(jnp.float32)

    def combine(lhs, rhs):
        a1, b1 = lhs
        a2, b2 = rhs
        return a1 * a2, a2 * b1 + b2

    _, h = lax.associative_scan(combine, (a, u), axis=1)
    return (jax.nn.gelu(y_in) * h.astype(x_in.dtype)) @ w_out


def setup_inputs(seed: int = 0) -> dict:
    key = jax.random.key(seed)
    keys = jax.random.split(key, 32)
    counter = [0]

    def next_key():
        k = keys[counter[0]]
        counter[0] += 1
        return k

    def nrm(shape, scale):
        return jax.random.normal(next_key(), shape, jnp.float32) * scale

    L = DEPTH
    d = D_MODEL
    x = nrm((BATCH, SEQ, d), 1.0)
    c = nrm((BATCH, d), 1.0)
    w_ada = nrm((L, d, 6 * d), 0.5 * d ** -0.5)
    b_ada = nrm((L, 6 * d), 0.02)
    g_norm_mix = 1.0 + nrm((L, d), 0.05)
    g_norm_ffn = 1.0 + nrm((L, d), 0.05)
    w_in = nrm((L, d, IN_COLS), d ** -0.5)
    conv_dw_w = nrm((L, CONV_KERNEL, CONV_WIDTH), CONV_KERNEL ** -0.5)
    conv_dw_b = nrm((L, CONV_WIDTH), 0.02)
    conv_ln_g = 1.0 + nrm((L, CONV_WIDTH), 0.05)
    conv_ln_b = nrm((L, CONV_WIDTH), 0.02)
    w_conv_out = nrm((L, CONV_WIDTH, d), CONV_WIDTH ** -0.5)
    fox_b_f = 2.0 + nrm((L, FOX_HEADS), 0.5)
    fox_q_norm_g = 1.0 + nrm((L, FOX_HEAD_DIM), 0.05)
    fox_k_norm_g = 1.0 + nrm((L, FOX_HEAD_DIM), 0.05)
    w_fox_out = nrm((L, FOX_WIDTH, d), FOX_WIDTH ** -0.5)
    lru_conv_w = nrm((L, LRU_CONV, LRU_WIDTH), LRU_CONV ** -0.5)
    lru_conv_b = nrm((L, LRU_WIDTH), 0.02)
    lru_w_r = nrm((L, LRU_BLOCKS, LRU_BLOCK_DIM, LRU_BLOCK_DIM), LRU_BLOCK_DIM ** -0.5)
    lru_b_r = nrm((L, LRU_WIDTH), 0.02)
    lru_w_i = nrm((L, LRU_BLOCKS, LRU_BLOCK_DIM, LRU_BLOCK_DIM), LRU_BLOCK_DIM ** -0.5)
    lru_b_i = nrm((L, LRU_WIDTH), 0.02)
    a_c = jax.random.uniform(next_key(), (L, LRU_WIDTH), jnp.float32, 0.9, 0.999)
    a0 = a_c ** (1.0 / LRU_C)
    lru_lambda = jnp.log(a0) - jnp.log1p(-a0)
    w_lru_out = nrm((L, LRU_WIDTH, d), LRU_WIDTH ** -0.5)
    w_o = nrm((L, d, d), d ** -0.5)
    w_ffn_in = nrm((L, d, 2 * FFN_HIDDEN), d ** -0.5)
    w_ffn_out = nrm((L, FFN_HIDDEN, d), FFN_HIDDEN ** -0.5)
    return {
        "x": x, "c": c, "w_ada": w_ada, "b_ada": b_ada,
        "g_norm_mix": g_norm_mix, "g_norm_ffn": g_norm_ffn, "w_in": w_in,
        "conv_dw_w": conv_dw_w, "conv_dw_b": conv_dw_b, "conv_ln_g": conv_ln_g,
        "conv_ln_b": conv_ln_b, "w_conv_out": w_conv_out,
        "fox_b_f": fox_b_f, "fox_q_norm_g": fox_q_norm_g, "fox_k_norm_g": fox_k_norm_g,
        "w_fox_out": w_fox_out,
        "lru_conv_w": lru_conv_w, "lru_conv_b": lru_conv_b, "lru_w_r": lru_w_r,
        "lru_b_r": lru_b_r, "lru_w_i": lru_w_i, "lru_b_i": lru_b_i,
        "lru_lambda": lru_lambda, "w_lru_out": w_lru_out,
        "w_o": w_o, "w_ffn_in": w_ffn_in, "w_ffn_out": w_ffn_out,
    }


def reference(x, c, w_ada, b_ada, g_norm_mix, g_norm_ffn, w_in,
              conv_dw_w, conv_dw_b, conv_ln_g, conv_ln_b, w_conv_out,
              fox_b_f, fox_q_norm_g, fox_k_norm_g, w_fox_out,
              lru_conv_w, lru_conv_b, lru_w_r, lru_b_r, lru_w_i, lru_b_i,
              lru_lambda, w_lru_out, w_o, w_ffn_in, w_ffn_out):
    bsz, seq, _ = x.shape
    assert seq % CHUNK == 0 and seq % Q_BLOCK == 0
    split_at = [int(s) for s in np.cumsum(IN_SIZES)[:-1]]
    c_act = jax.nn.silu(c)
    for l in range(DEPTH):
        mod = c_act @ w_ada[l] + b_ada[l]
        sh1, sc1, gt1, sh2, sc2, gt2 = [m[:, None, :] for m in jnp.split(mod, 6, axis=-1)]

        h = rms_norm(x, g_norm_mix[l]) * (1.0 + sc1) + sh1
        z = h @ w_in[l]
        u_glu, q, k, v, f_logit, y_in, x_in, g_logit = jnp.split(z, split_at, axis=-1)
        br_a = conformer_conv_branch(u_glu, conv_dw_w[l], conv_dw_b[l],
                                     conv_ln_g[l], conv_ln_b[l], w_conv_out[l])
        br_b = forgetting_attention_branch(q, k, v, f_logit, fox_b_f[l],
                                           fox_q_norm_g[l], fox_k_norm_g[l], w_fox_out[l])
        br_c = rg_lru_branch(y_in, x_in, lru_conv_w[l], lru_conv_b[l], lru_w_r[l], lru_b_r[l],
                             lru_w_i[l], lru_b_i[l], lru_lambda[l], w_lru_out[l])
        gates = jax.nn.sigmoid(g_logit).reshape(bsz, seq, N_BRANCH, D_MODEL)
        merged = gates[:, :, 0] * br_a + gates[:, :, 1] * br_b + gates[:, :, 2] * br_c
        x = x + gt1 * (merged @ w_o[l])

        h = rms_norm(x, g_norm_ffn[l]) * (1.0 + sc2) + sh2
        a_ffn, b_ffn = jnp.split(h @ w_ffn_in[l], 2, axis=-1)
        x = x + gt2 * ((jax.nn.silu(a_ffn) * b_ffn) @ w_ffn_out[l])
    return x
```

```python
import numpy as np
from contextlib import ExitStack
import concourse.bass as bass
import concourse.mybir as mybir
from concourse.bass_utils import run_bass_kernel_spmd

F32 = mybir.dt.float32
BF16 = mybir.dt.bfloat16
AF = mybir.ActivationFunctionType
ALU = mybir.AluOpType
DT_SIZE = {mybir.dt.float32: 4, mybir.dt.bfloat16: 2}

T = 2048
D = 1024
TB = 512
NTB = 4
NLAYERS = 4
EPS = 1e-6
PADV = 32
ENGS = ("pe", "act", "dve", "pool", "sp")

BADA, GN1, GN2, CW, CB, CLG, CLB, GQ, GK, BF_, LW, LB, LBR, LBI, LAM, NPV = (
    0, 48, 56, 64, 188, 192, 196, 200, 201, 202, 203, 219, 223, 227, 231, 236)


class Op:
    __slots__ = ("eng", "fn", "deps", "is_dma", "sig", "semval", "dsem", "dval", "prev_dma")

    def __init__(self, eng, fn, is_dma):
        self.eng = eng
        self.fn = fn
        self.deps = set()
        self.is_dma = is_dma
        self.sig = False
        self.semval = 0
        self.dsem = None
        self.dval = 0
        self.prev_dma = None


class Sched:
    def __init__(self, nc, n_dma_sems=16):
        self.nc = nc
        self.streams = {e: [] for e in ENGS}
        self.lastw = {}
        self.readers = {}
        self.n_dma_sems = n_dma_sems
        self.dma_count = 0
        self.dma_sem_last = [None] * n_dma_sems
        self.dma_sem_uses = [0] * n_dma_sems

    @staticmethod
    def blocks(ap):
        space = str(ap.space)
        if space not in ("SB", "PSUM"):
            return ()
        dims = ap.ap
        esz = DT_SIZE[ap.dtype]
        pstride = dims[0][0]
        off = ap.offset
        col = off % pstride if pstride > 0 else off
        ext = 0
        for st, cnt in dims[1:]:
            if cnt > 1:
                ext += (cnt - 1) * abs(st)
        lo = col * esz
        hi = (col + ext + 1) * esz
        bs = 512 if space == "SB" else 256
        name = ap.tensor.name
        return [(name, b) for b in range(lo // bs, (hi - 1) // bs + 1)]

    def add(self, eng, fn, reads=(), writes=(), dma=False):
        op = Op(eng, fn, dma)
        deps = op.deps
        for ap in reads:
            for k in self.blocks(ap):
                w = self.lastw.get(k)
                if w is not None:
                    deps.add(w)
                self.readers.setdefault(k, []).append(op)
        for ap in writes:
            for k in self.blocks(ap):
                w = self.lastw.get(k)
                if w is not None:
                    deps.add(w)
                for r in self.readers.get(k, ()):
                    deps.add(r)
                self.lastw[k] = op
                self.readers[k] = []
        deps.discard(op)
        if dma:
            s = self.dma_count % self.n_dma_sems
            self.dma_count += 1
            op.prev_dma = self.dma_sem_last[s]
            self.dma_sem_last[s] = op
            self.dma_sem_uses[s] += 1
            op.dsem = s
            op.dval = 16 * self.dma_sem_uses[s]
        self.streams[eng].append(op)
        return op

    def emit(self, final_wait_ops=()):
        nc = self.nc
        for e in ENGS:
            for op in self.streams[e]:
                for d in op.deps:
                    if d.is_dma:
                        continue
                    if d.eng == "pe" and op.eng == "pe" and not op.is_dma:
                        continue
                    d.sig = True
        for e in ENGS:
            c = 0
            for op in self.streams[e]:
                if not op.is_dma and op.sig:
                    c += 1
                    op.semval = c
        with ExitStack() as es:
            esem = {e: es.enter_context(nc.semaphore("s_" + e)) for e in ("pe", "act", "dve", "pool", "sp")}
            dsems = [es.enter_context(nc.semaphore("d%d" % i)) for i in range(self.n_dma_sems)]
            block = es.enter_context(nc.Block())

            def run(ename, eng):
                seen = {}
                for op in self.streams[ename]:
                    need = {}
                    for d in op.deps:
                        if d.is_dma:
                            k = ("d", d.dsem)
                            need[k] = max(need.get(k, 0), d.dval)
                        else:
                            if d.eng == "pe" and ename == "pe" and not op.is_dma:
                                continue
                            k = ("e", d.eng)
                            need[k] = max(need.get(k, 0), d.semval)
                    if op.is_dma and op.prev_dma is not None:
                        k = ("d", op.dsem)
                        need[k] = max(need.get(k, 0), op.prev_dma.dval)
                    for k, v in need.items():
                        if seen.get(k, 0) >= v:
                            continue
                        seen[k] = v
                        eng.wait_ge(dsems[k[1]] if k[0] == "d" else esem[k[1]], v)
                    ins = op.fn(eng)
                    if op.is_dma:
                        ins.then_inc(dsems[op.dsem], 16)
                    elif op.sig:
                        ins.then_inc(esem[ename], 1)
                if ename == "sp":
                    for op in final_wait_ops:
                        eng.wait_ge(dsems[op.dsem], op.dval)

            @block.tensor
            def _(e):
                run("pe", e)

            @block.scalar
            def _(e):
                run("act", e)

            @block.vector
            def _(e):
                run("dve", e)

            @block.gpsimd
            def _(e):
                run("pool", e)

            @block.sync
            def _(e):
                run("sp", e)


def build(NL=NLAYERS, dbg=False):
    nc = bass.Bass("TRN2", target_bir_lowering=False)

    def dram(name, shape, dt=F32, kind="ExternalInput"):
        return nc.dram_tensor(name, shape, dt, kind=kind).ap()

    x_d = dram("x", [T, D])
    c_d = dram("c", [128, 8])
    wada_d = dram("wada", [NL, 24, 128, 2048])
    win_d = dram("win", [NL, 14, 128, 2048])
    wg2_d = dram("wg2", [NL, 8, 128, 2048])
    wg1_d = dram("wg1", [NL, 8, 128, 1024])
    wf_d = dram("wf", [NL, 128, 128])
    bd_d = dram("bd", [NL, 128, 1024])
    pout_d = dram("pout", [NL, 8, 128, 1536])
    wo_d = dram("wo", [NL, 4, 128, 2048])
    wfi_d = dram("wfi", [NL, 22, 128, 2048])
    wfo_d = dram("wfo", [NL, 16, 128, 1408])
    pv_d = dram("pv", [NL, 128, NPV])
    out_d = dram("out", [T, D], kind="ExternalOutput")
    dbg_d = {}
    dbg_ops = []
    if dbg:
        for nm in ("oT", "yA", "yC"):
            dbg_d[nm] = dram("dbg_" + nm, [128, 4 * T], BF16, kind="ExternalOutput")
        dbg_d["xmid"] = dram("dbg_xmid", [128, 8 * T], F32, kind="ExternalOutput")
        dbg_d["hT"] = dram("dbg_hT", [128, 8 * T], BF16, kind="ExternalOutput")

    es = ExitStack()

    def sb(name, shape, dt):
        return es.enter_context(nc.sbuf_tensor(name, shape, dt))

    xT = sb("xT", [128, 8 * T], F32)
    hT = sb("hT", [128, 8 * T], BF16)
    ARENA = 74240
    ar = sb("arena", [128, ARENA // 2], BF16)
    ws = [sb("ws%d" % i, [128, 2048], BF16) for i in range(4)]
    tmpbig = sb("tmpbig", [128, 6 * 512], F32)

    class _Sl:
        def __init__(self, ap):
            self.ap = ap

        def __getitem__(self, idx):
            return self.ap if idx == slice(None) else self.ap[idx]

    tmp = [_Sl(tmpbig[:, i * 512:(i + 1) * 512]) for i in range(6)]
    identf = sb("identf", [128, 128], F32)
    identb = sb("identb", [128, 128], BF16)
    onesb = sb("onesb", [128, 128], BF16)
    blockones = sb("blockones", [128, 128], BF16)
    maskb = sb("maskb", [128, 128], BF16)
    negmask = sb("negmask", [128, 128], BF16)
    selb = sb("selb", [16, 8 * 128], BF16)
    zerob = sb("zerob", [128, 272], BF16)
    pvts = [sb("pvt%d" % i, [128, NPV], F32) for i in range(2)]
    dvs = [sb("dv%d" % i, [128, 128], F32) for i in range(2)]
    cact = sb("cact", [128, 8], BF16)
    cf = sb("cf", [128, 8], F32)
    small = sb("small", [128, 64], F32)
    rcs = sb("rcs", [128, 8], F32)
    otoks = [sb("otok%d" % i, [128, 512], BF16) for i in range(2)]
    lrs = sb("lrs", [128, 8], F32)
    negcumT = sb("negcumT", [128, 128], F32)
    ps = [es.enter_context(nc.psum_tensor("ps%d" % i, [128, 512], F32)) for i in range(8)]

    S = Sched(nc)

    def arv(off, nbytes, dt=BF16):
        a = ar[:, off // 2:(off + nbytes) // 2]
        if dt == F32:
            a = a.bitcast(F32)
        return a

    def isap(v):
        return not isinstance(v, (int, float)) and v is not None

    def mm(out, lhsT, rhs, start=True, stop=True, skip=False):
        return S.add("pe", lambda e: e.matmul(out, lhsT=lhsT, rhs=rhs, start=start, stop=stop,
                                              skip_group_check=skip), reads=[lhsT, rhs], writes=[out])

    def tr(out, in_, ident):
        return S.add("pe", lambda e: e.transpose(out=out, in_=in_, identity=ident), reads=[in_, ident], writes=[out])

    def act(out, in_, func, bias=None, scale=None):
        kw = {}
        rd = [in_]
        if bias is not None:
            kw["bias"] = bias
            if isap(bias):
                rd.append(bias)
        if scale is not None:
            kw["scale"] = scale
            if isap(scale):
                rd.append(scale)
        return S.add("act", lambda e: e.activation(out=out, in_=in_, func=func, **kw), reads=rd, writes=[out])

    def tt(out, in0, in1, op, eng="dve"):
        return S.add(eng, lambda e: e.tensor_tensor(out=out, in0=in0, in1=in1, op=op), reads=[in0, in1], writes=[out])

    def tsc(out, in0, s1, s2, op0, op1=None, eng="dve"):
        rd = [in0] + [s for s in (s1, s2) if isap(s)]
        if op1 is None:
            return S.add(eng, lambda e: e.tensor_scalar(out=out, in0=in0, scalar1=s1, scalar2=None, op0=op0), reads=rd, writes=[out])
        return S.add(eng, lambda e: e.tensor_scalar(out=out, in0=in0, scalar1=s1, scalar2=s2, op0=op0, op1=op1), reads=rd, writes=[out])

    def stt(out, in0, scalar, in1, op0, op1):
        rd = [in0, in1] + ([scalar] if isap(scalar) else [])
        return S.add("dve", lambda e: e.scalar_tensor_tensor(out=out, in0=in0, scalar=scalar, in1=in1, op0=op0, op1=op1), reads=rd, writes=[out])

    def cp(out, in_, eng="dve"):
        if eng == "act":
            return S.add("act", lambda e: e.copy(out=out, in_=in_), reads=[in_], writes=[out])
        return S.add(eng, lambda e: e.tensor_copy(out=out, in_=in_), reads=[in_], writes=[out])

    def recip(out, in_):
        return S.add("dve", lambda e: e.reciprocal(out=out, in_=in_), reads=[in_], writes=[out])

    def mset(ap, val, eng="pool"):
        return S.add(eng, lambda e: e.memset(ap, val), writes=[ap])

    def dma(out, in_, eng="sp", **kw):
        return S.add(eng, lambda e: e.dma_start(out=out, in_=in_, **kw), reads=[in_], writes=[out], dma=True)

    wctr = [0]

    def wload(src, n):
        slot = ws[wctr[0] % 4]
        wctr[0] += 1
        dst = slot[:, 0:n]
        S.add("pool", lambda e: e.dma_start(out=dst, in_=src, max_dma_last_dim=4096), reads=[], writes=[dst], dma=True)
        return slot

    pctr = {}
    default_pool = [tuple(range(7))]

    def bank(pool=None):
        pool = default_pool[0] if pool is None else tuple(pool)
        n = pctr.get(pool, 0)
        pctr[pool] = n + 1
        return ps[pool[n % len(pool)]]

    mset(identf[:], 0.0)
    S.add("pool", lambda e: e.affine_select(out=identf[:], in_=identf[:], pattern=[[-1, 128]], compare_op=ALU.not_equal,
                                            fill=1.0, base=0, channel_multiplier=1), reads=[identf[:]], writes=[identf[:]])
    cp(identb[:], identf[:], eng="pool")
    mset(onesb[:], 1.0)
    mset(zerob[:], 0.0)
    mset(blockones[:], 0.0)
    mset(blockones[0:64, 0:64], 1.0)
    mset(blockones[64:128, 64:128], 1.0)
    S.add("pool", lambda e: e.affine_select(out=maskb[:], in_=onesb[:], pattern=[[1, 128]], compare_op=ALU.is_ge,
                                            fill=0.0, base=0, channel_multiplier=-1), reads=[onesb[:]], writes=[maskb[:]])
    tsc(negmask[:], maskb[:], -1.0, 30000.0, ALU.add, ALU.mult, eng="pool")
    mset(selb[:], 0.0)
    selv = selb[:].rearrange("p (h m) -> p h m", h=8)
    S.add("pool", lambda e: e.affine_select(out=selv, in_=selv, pattern=[[-1, 8], [0, 128]], compare_op=ALU.not_equal,
                                            fill=1.0, base=0, channel_multiplier=1), reads=[selb[:]], writes=[selb[:]])
    S.add("pool", lambda e: e.affine_select(out=selv, in_=selv, pattern=[[-1, 8], [0, 128]], compare_op=ALU.not_equal,
                                            fill=1.0, base=-8, channel_multiplier=1), reads=[selb[:]], writes=[selb[:]])
    mset(small[:, 0:1], 1.0)
    mset(small[:, 1:2], EPS)
    mset(small[:, 4:5], 0.25)
    mset(small[:, 2:4], 0.0)
    mset(small[0:8, 2:3], 1.0)
    mset(small[:, 3:4], 1.0)
    mset(small[0:8, 3:4], 0.0)
    ones_col = small[:, 0:1]
    eps_col = small[:, 1:2]

    xTv = xT[:].rearrange("p (c t) -> p c t", c=8)
    hTv = hT[:].rearrange("p (c t) -> p c t", c=8)
    xios = [ar[:, i * 2048:(i + 1) * 2048].bitcast(F32) for i in range(4)]
    for tt_i in range(16):
        xio = xios[tt_i % 4]
        dma(xio, x_d[tt_i * 128:(tt_i + 1) * 128, :])
        for half in range(2):
            pb = bank()
            for q in range(4):
                kc = half * 4 + q
                tr(pb[:, q * 128:(q + 1) * 128], xio[:, kc * 128:(kc + 1) * 128], identf[:])
            dst = xTv[:, half * 4:half * 4 + 4, tt_i * 128:(tt_i + 1) * 128]
            src = pb[:].rearrange("p (q t) -> p q t", q=4)
            if half == 0:
                cp(dst, src, eng="dve")
            else:
                cp(dst, src, eng="act")

    dma(cf[:], c_d)
    act(cact[:], cf[:], AF.Silu)

    scr_tmp = ([tmp[0][:], tmp[1][:]],
               [tmp[5][:].bitcast(BF16)[:, 0:512], tmp[5][:].bitcast(BF16)[:, 512:1024]],
               [tmp[2][:], tmp[3][:], tmp[4][:]])
    scr_ffn = ([arv(45056, 2048, F32), arv(47104, 2048, F32)],
               [arv(49152, 1024), arv(50176, 1024)],
               [arv(51200, 2048, F32), arv(53248, 2048, F32), arv(55296, 2048, F32)])

    def norm_stats(tb, scr=None):
        rstds, sqs, _ = scr or scr_tmp
        pb = bank()
        for kc in range(8):
            s_ = sqs[kc % 2]
            xs = xTv[:, kc, tb * TB:(tb + 1) * TB]
            if kc % 2 == 0:
                act(s_, xs, AF.Square)
            else:
                tt(s_, xs, xs, ALU.mult)
            mm(pb[:], onesb[:], s_, start=(kc == 0), stop=(kc == 7))
        act(rstds[tb % 2], pb[:], AF.Ln, bias=eps_col, scale=1.0 / D)
        act(rstds[tb % 2], rstds[tb % 2], AF.Exp, scale=-0.5)

    def norm_apply(tb, dvt, Acol, Bcol, scr=None):
        rstds, _, xns = scr or scr_tmp
        for kc in range(8):
            xn = xns[kc % 3]
            tt(xn, xTv[:, kc, tb * TB:(tb + 1) * TB], rstds[tb % 2], ALU.mult)
            if kc % 4 == 3:
                tsc(hTv[:, kc, tb * TB:(tb + 1) * TB], xn, dvt[:, Acol + kc:Acol + kc + 1], dvt[:, Bcol + kc:Bcol + kc + 1], ALU.mult, ALU.add)
            else:
                act(hTv[:, kc, tb * TB:(tb + 1) * TB], xn, AF.Identity, bias=dvt[:, Bcol + kc:Bcol + kc + 1],
                    scale=dvt[:, Acol + kc:Acol + kc + 1])

    def norm_to_hT(dvt, Acol, Bcol):
        norm_stats(0)
        for tb in range(NTB):
            if tb + 1 < NTB:
                norm_stats(tb + 1)
            norm_apply(tb, dvt, Acol, Bcol)

    MOD = 0
    A1 = 48
    A2 = 56
    GQ8 = 64
    NEGBF = 65
    CL = 66
    CL2 = 70
    SCR = 80
    HBR = 100
    HBI = 104
    HCL = 108

    def ada_group(l_, g):
        slot = wload(wada_d[l_, g], 2048)
        sv = slot[:].rearrange("p (k n) -> p k n", k=8)
        for jj in range(2):
            j = 2 * g + jj
            for kc in range(8):
                mm(ps[7][:, j:j + 1], sv[:, kc, jj * 128:(jj + 1) * 128], cact[:, kc:kc + 1], start=(kc == 0), stop=(kc == 7), skip=True)

    def finalize_params(pvt, dv, part=0):
        pm = ps[7]
        if part in (0, 1):
            tt(dv[:, MOD:MOD + 16], pm[:, 0:16], pvt[:, BADA:BADA + 16], ALU.add)
            stt(dv[:, A1:A1 + 8], dv[:, MOD + 8:MOD + 16], 1.0, pvt[:, GN1:GN1 + 8], ALU.add, ALU.mult)
            if part == 1:
                return
        tt(dv[:, MOD + 16:MOD + 48], pm[:, 16:48], pvt[:, BADA + 16:BADA + 48], ALU.add)
        stt(dv[:, A2:A2 + 8], dv[:, MOD + 32:MOD + 40], 1.0, pvt[:, GN2:GN2 + 8], ALU.add, ALU.mult)
        tsc(dv[:, GQ8:GQ8 + 1], pvt[:, GQ:GQ + 1], 0.125, None, ALU.mult)
        tsc(dv[:, NEGBF:NEGBF + 1], pvt[:, BF_:BF_ + 1], -1.0, None, ALU.mult)
        e_ = dv[:, SCR:SCR + 4]
        w_ = dv[:, SCR + 4:SCR + 8]
        d_ = dv[:, SCR + 8:SCR + 12]
        l_ = dv[:, SCR + 12:SCR + 16]
        act(e_, pvt[:, LAM:LAM + 4], AF.Exp, scale=-1.0)
        tsc(w_, e_, 1.0, None, ALU.add)
        tsc(d_, w_, -1.0, None, ALU.add)
        recip(d_, d_)
        act(l_, w_, AF.Ln)
        tt(l_, l_, e_, ALU.mult)
        tt(l_, l_, d_, ALU.mult)
        tsc(dv[:, CL:CL + 4], l_, -8.0, None, ALU.mult)
        tsc(dv[:, CL2:CL2 + 4], l_, -16.0, None, ALU.mult)
        tsc(dv[:, HCL:HCL + 4], l_, -4.0, None, ALU.mult)
        tsc(dv[:, HBR:HBR + 4], pvt[:, LBR:LBR + 4], 0.5, None, ALU.mult)
        tsc(dv[:, HBI:HBI + 4], pvt[:, LBI:LBI + 4], 0.5, None, ALU.mult)

    for l in range(NL):
        pvt = pvts[l % 2]
        dv = dvs[l % 2]
        if l == 0:
            dma(pvt[:], pv_d[0])
            for g in range(8):
                ada_group(0, g)
            finalize_params(pvt, dv, part=1)
            rest = list(range(8, 24))
            norm_stats(0)
            for tb in range(NTB):
                for _ in range(2):
                    ada_group(0, rest.pop(0))
                if tb + 1 < NTB:
                    norm_stats(tb + 1)
                norm_apply(tb, dv, A1, MOD + 0)
                for _ in range(2):
                    ada_group(0, rest.pop(0))
            assert not rest
            finalize_params(pvt, dv, part=2)
            default_pool[0] = tuple(range(8))
        else:
            default_pool[0] = tuple(range(8))
        if dbg and l == 0:
            dbg_ops.append(dma(dbg_d["hT"], hT[:]))

        oT = arv(0, 16384).rearrange("p (c t) -> p c t", c=4)
        Vaug = arv(16384, 16640).rearrange("p (j h d) -> p j h d", j=16, h=8)
        qkset = [[arv(base + s_ * 4096, 4096) for s_ in range(4)] for base in (33280, 57856)]
        cumHL = arv(49664, 4096)
        PT = [arv(53760 + i * 1024, 1024) for i in range(4)]
        hi_bf = arv(53760, 4096)
        bufA = arv(57856, 8192, F32)
        bufB = arv(66048, 8192, F32)

        mset(Vaug[:, :, :, 64:65], 1.0, eng="dve")
        wfslot = wload(wf_d[l], 128)
        wfv = wfslot[:, 0:128].rearrange("p (k n) -> p k n", k=8)
        for tb in range(NTB):
            pb = bank()
            for kc in range(8):
                mm(pb[0:16, :], wfv[:, kc, :], hTv[:, kc, tb * TB:(tb + 1) * TB], start=(kc == 0), stop=(kc == 7))
            act(bufA[0:16, tb * TB:(tb + 1) * TB], pb[0:16, :], AF.Exp, bias=dv[0:16, NEGBF:NEGBF + 1], scale=-1.0)
        act(bufA[0:16, :], bufA[0:16, :], AF.Ln, bias=ones_col[0:16, :], scale=1.0)
        S.add("dve", lambda e: e.tensor_tensor_scan(out=bufB[0:16, :], data0=ones_col[0:16, :].to_broadcast([16, T]),
                                                    data1=bufA[0:16, :], initial=0.0, op0=ALU.mult, op1=ALU.add),
              reads=[bufA[0:16, :], ones_col[0:16, :]], writes=[bufB[0:16, :]])
        pb = bank()
        for j in range(16):
            tr(pb[:, j * 8:(j + 1) * 8], bufB[0:8, j * 128:(j + 1) * 128], identf[0:8, 0:8])
        cp(negcumT[:], pb[:, 0:128])
        tsc(bufA[0:16, :], bufB[0:16, :], -1.0, None, ALU.mult)
        cp(hi_bf[0:16, :], bufA[0:16, :])
        tt(bufB[0:16, :], bufA[0:16, :], hi_bf[0:16, :], ALU.subtract)
        tsc(bufA[0:16, :], hi_bf[0:16, :], small[0:16, 2:3], None, ALU.mult)
        stt(cumHL[0:16, :], bufB[0:16, :], small[0:16, 3:4], bufA[0:16, :], ALU.mult, ALU.add)
        for st in qkset:
            qzA, qzB, kzA, kzB = st
            mset(qzA[64:128, :].bitcast(F32), 0.0, eng="dve")
            mset(kzA[64:128, :].bitcast(F32), 0.0, eng="dve")
            mset(qzB[0:64, :].bitcast(F32), 0.0, eng="dve")
            mset(kzB[0:64, :].bitcast(F32), 0.0, eng="dve")
            cp(qzA[64:80, :], cumHL[0:16, :])
            cp(qzB[0:16, :], cumHL[0:16, :])

        def qk_proj(c):
            slot = wload(win_d[l, c], 2048)
            sv = slot[:].rearrange("p (k n) -> p k n", k=8)
            qzA, qzB, kzA, kzB = qkset[c % 2]
            cp(kzA[64:80, :].rearrange("p (j m) -> p j m", j=16),
               selb[0:16, (2 * c) * 128:(2 * c + 1) * 128].unsqueeze(1).to_broadcast([16, 16, 128]))
            cp(kzB[0:16, :].rearrange("p (j m) -> p j m", j=16),
               selb[0:16, (2 * c + 1) * 128:(2 * c + 2) * 128].unsqueeze(1).to_broadcast([16, 16, 128]))
            for s_i, (dA, dB, gcol) in enumerate(((qzA, qzB, dv[:, GQ8:GQ8 + 1]), (kzA, kzB, pvt[:, GK:GK + 1]))):
                for tb in range(NTB):
                    pb = bank((0, 1, 2, 3, 4))
                    for kc in range(8):
                        mm(pb[:], sv[:, kc, s_i * 128:(s_i + 1) * 128], hTv[:, kc, tb * TB:(tb + 1) * TB], start=(kc == 0), stop=(kc == 7))
                    sq = tmp[5][:].bitcast(BF16)[:, (tb % 2) * 512:(tb % 2) * 512 + 512]
                    act(sq, pb[:], AF.Square)
                    pb2 = bank((7,))
                    mm(pb2[:], blockones[:], sq)
                    rstd = tmp[tb % 4][:]
                    act(rstd, pb2[:], AF.Ln, bias=eps_col, scale=1.0 / 64.0)
                    act(rstd, rstd, AF.Exp, scale=-0.5)
                    stt(dA[0:64, tb * TB:(tb + 1) * TB], pb[0:64, :], gcol[0:64, :], rstd[0:64, :], ALU.mult, ALU.mult)
                    stt(dB[64:128, tb * TB:(tb + 1) * TB], pb[64:128, :], gcol[64:128, :], rstd[64:128, :], ALU.mult, ALU.mult)

        qk_proj(0)

        vs = [wload(win_d[l, 4], 2048), wload(win_d[l, 5], 2048)]
        vsv = [s_[:].rearrange("p (k n) -> p k n", k=8) for s_ in vs]
        vdone = set()

        def v_proj(tt_i):
            vdone.add(tt_i)
            pb = bank((0, 1, 2, 3, 4))
            for g in range(2):
                for kc in range(8):
                    mm(pb[:, g * 256:(g + 1) * 256], hTv[:, kc, tt_i * 128:(tt_i + 1) * 128], vsv[g][:, kc, :],
                       start=(kc == 0), stop=(kc == 7), skip=True)
            src = pb[:].rearrange("p (h d) -> p h d", h=8)
            cp(Vaug[:, tt_i, :, 0:64], src, eng="dve")

        def attn(c):
            qz = qkset[c % 2][0:2]
            kz = qkset[c % 2][2:4]
            tasks = [(qb, hh, j) for qb in range(4) for hh in range(2) for j in range(4 * qb + 4)]
            LA = 3
            info = {}

            def emit_score(idx):
                qb, hh, j = tasks[idx]
                h = 2 * c + hh
                if j not in vdone:
                    v_proj(j)
                r = max(0, j - 4 * qb)
                n = TB - 128 * r
                q0 = qb * TB + 128 * r
                sp_ = bank((0, 1, 2, 3, 4))
                diagt = (j >= 4 * qb)
                mm(sp_[:, 0:n], kz[hh][:, j * 128:(j + 1) * 128], qz[hh][:, q0:q0 + n], start=True, stop=not diagt)
                if diagt:
                    mm(sp_[:, 0:128], identb[:], negmask[:], start=False, stop=True)
                pt = PT[idx % 4]
                act(pt[:, 0:n], sp_[:, 0:n], AF.Exp, bias=negcumT[:, j * 8 + h:j * 8 + h + 1], scale=1.0)
                info[idx] = (pt, r)

            def emit_pv(idx):
                qb, hh, j = tasks[idx]
                h = 2 * c + hh
                bp = 64 * hh
                pt, r = info.pop(idx)
                acc = ps[5 + (qb * 2 + hh) % 2]
                if j == 0:
                    mm(acc[:, 0:260], zerob[:, 0:128], zerob[:, 0:260], start=True, stop=False, skip=True)
                accv = acc[:, 0:260].rearrange("p (i d) -> p i d", i=4)
                for ii in range(r, 4):
                    mm(accv[:, ii, :], pt[:, (ii - r) * 128:(ii - r + 1) * 128], Vaug[:, j, h, :], start=False,
                       stop=(j == 4 * qb + ii), skip=True)
                if j == 4 * qb + 3:
                    otok = otoks[qb % 2][:].rearrange("p (i d) -> p i d", i=4)
                    rc = rcs[:, hh * 4:4 + hh * 4]
                    recip(rc, accv[:, :, 64])
                    tt(otok[:, :, bp:bp + 64], accv[:, :, 0:64], rc.unsqueeze(2).to_broadcast([128, 4, 64]), ALU.mult)
                    if hh == 1:
                        pbt = ps[7]
                        pbtv = pbt[:].bitcast(BF16)[:, 0:512]
                        for ii in range(4):
                            tr(pbtv[:, ii * 128:(ii + 1) * 128], otok[:, ii, :], identb[:])
                        cp(oT[:, c, qb * TB:(qb + 1) * TB], pbtv, eng="dve")

            for idx in range(len(tasks) + LA):
                if idx < len(tasks):
                    emit_score(idx)
                if idx >= LA:
                    emit_pv(idx - LA)

        for c in range(4):
            if c + 1 < 4:
                qk_proj(c + 1)
            attn(c)
        if dbg and l == 0:
            dbg_ops.append(dma(dbg_d["oT"], arv(0, 16384)))

        vpad = arv(16384, 16640).rearrange("p (c t) -> p c t", c=4)
        yA = vpad[:, :, PADV:PADV + T]
        Cc = arv(33280, 32768, F32).rearrange("p (c t) -> p c t", c=4)
        diag = arv(66048, 7936).rearrange("p (k n) -> p k n", k=31)
        mset(vpad[:, :, 0:PADV], 0.0)
        def conv_ln(tb):
            p1 = bank()
            p2 = bank()
            for i in range(4):
                cb = tmp[4][:].bitcast(BF16)[:, 0:512]
                cq = tmp[4][:].bitcast(BF16)[:, 512:1024]
                cp(cb, Cc[:, i, tb * TB:(tb + 1) * TB], eng="dve")
                act(cq, Cc[:, i, tb * TB:(tb + 1) * TB], AF.Square)
                mm(p1[:], onesb[:], cb, start=(i == 0), stop=(i == 3))
                mm(p2[:], onesb[:], cq, start=(i == 0), stop=(i == 3))
            mean = tmp[0][:]
            var = tmp[1][:]
            act(mean, p1[:], AF.Identity, scale=1.0 / 512.0)
            tt(var, mean, mean, ALU.mult)
            stt(var, p2[:], 1.0 / 512.0, var, ALU.mult, ALU.subtract)
            act(var, var, AF.Ln, bias=eps_col, scale=1.0)
            act(var, var, AF.Exp, scale=-0.5)
            for i in range(4):
                t1 = tmp[2 + i % 2][:]
                tt(t1, Cc[:, i, tb * TB:(tb + 1) * TB], mean, ALU.subtract)
                tt(t1, t1, var, ALU.mult)
                act(yA[:, i, tb * TB:(tb + 1) * TB], t1, AF.Silu, bias=pvt[:, CLB + i:CLB + i + 1], scale=pvt[:, CLG + i:CLG + i + 1])

        for i in range(4):
            slot = wload(win_d[l, 6 + i], 2048)
            sv = slot[:].rearrange("p (k n) -> p k n", k=8)
            for tb in range(NTB):
                pg = bank()
                pa = bank()
                for kc in range(8):
                    mm(pg[:], sv[:, kc, 0:128], hTv[:, kc, tb * TB:(tb + 1) * TB], start=(kc == 0), stop=(kc == 7))
                for kc in range(8):
                    mm(pa[:], sv[:, kc, 128:256], hTv[:, kc, tb * TB:(tb + 1) * TB], start=(kc == 0), stop=(kc == 7))
                sg = tmp[tb % 2][:]
                act(sg, pg[:], AF.Sigmoid)
                tt(vpad[:, i, PADV + tb * TB:PADV + (tb + 1) * TB], pa[:], sg, ALU.mult)
            for k in range(31):
                tsc(diag[:, k, :], identb[:], pvt[:, CW + i * 31 + k:CW + i * 31 + k + 1], None, ALU.mult)
            for tb in range(NTB):
                pc = bank()
                for k in range(31):
                    mm(pc[:], diag[:, k, :], vpad[:, i, tb * TB + k + 2:tb * TB + k + 2 + TB], start=(k == 0), stop=(k == 30))
                act(Cc[:, i, tb * TB:(tb + 1) * TB], pc[:], AF.Identity, bias=pvt[:, CB + i:CB + i + 1], scale=1.0)
                if i == 3 and tb >= 1:
                    conv_ln(tb - 1)
        conv_ln(NTB - 1)
        if dbg and l == 0:
            for i in range(4):
                dbg_ops.append(dma(dbg_d["yA"][:, i * T:(i + 1) * T], yA[:, i, :]))

        yC = arv(33280, 16384).rearrange("p (c t) -> p c t", c=4)
        QT = 512
        Xbs = [[arv(49664, 2064, F32), arv(69120, 2064, F32)], [arv(52224, 2064, F32), arv(71680, 2064, F32)]]
        XRs = [[arv(54784, 2048, F32), tmp[2][:]], [arv(56832, 2048, F32), tmp[3][:]]]
        RAs = [arv(58880, 2048, F32), arv(60928, 2048, F32)]
        MHs = [arv(62976, 2048, F32), arv(65024, 2048, F32)]
        xrbs = [[arv(67072, 1024), tmp[4][:].bitcast(BF16)[:, 0:512]], [arv(68096, 1024), tmp[4][:].bitcast(BF16)[:, 512:1024]]]
        gys = [tmp[0][:], tmp[1][:]]
        hlasts = [lrs[:, 0:1], lrs[:, 1:2]]
        halos = [lrs[:, 2:5], lrs[:, 5:8]]
        lru_w = {}

        def lru_A(it):
            pr, q = it // 4, it % 4
            b = it % 2
            if q == 0:
                sl = [wload(win_d[l, 10 + 2 * pr + s_], 2048) for s_ in range(2)]
                bdslot = wload(bd_d[l][:, 512 * pr:512 * pr + 512], 512)
                lru_w[pr] = ([x_[:].rearrange("p (k n) -> p k n", k=8) for x_ in sl],
                             bdslot[:, 0:512].rearrange("p (i g n) -> p i g n", i=2, g=2))
            svs, bdv = lru_w[pr]
            for s_ in range(2):
                Xb = Xbs[s_][b]
                if q == 0:
                    mset(Xb[:, 0:3], 0.0, eng="dve")
                else:
                    cp(Xb[:, 0:3], halos[s_])
            for s_ in range(2):
                Xb = Xbs[s_][b]
                px = bank()
                for kc in range(8):
                    mm(px[:], svs[s_][:, kc, 0:128], hTv[:, kc, q * QT:(q + 1) * QT], start=(kc == 0), stop=(kc == 7))
                cp(Xb[:, 3:3 + QT], px[:], eng="act")
            for k in range(4):
                for s_ in range(2):
                    i = 2 * pr + s_
                    Xb, XR = Xbs[s_][b], XRs[s_][b]
                    if k == 0:
                        tsc(XR, Xb[:, 0:QT], pvt[:, LW + i * 4:LW + i * 4 + 1], pvt[:, LB + i:LB + i + 1], ALU.mult, ALU.add)
                    else:
                        stt(XR, Xb[:, k:k + QT], pvt[:, LW + i * 4 + k:LW + i * 4 + k + 1], XR, ALU.mult, ALU.add)
            for s_ in range(2):
                cp(xrbs[s_][b], XRs[s_][b], eng="dve")
                if q < 3:
                    cp(halos[s_], Xbs[s_][b][:, QT:QT + 3])

        def lru_B(it):
            pr, q = it // 4, it % 4
            b = it % 2
            svs, bdv = lru_w[pr]
            two = range(2)
            Ibs = [Xbs[s_][b][:, 4:4 + QT] for s_ in two]
            for s_ in two:
                py = bank()
                for kc in range(8):
                    mm(py[:], svs[s_][:, kc, 128:256], hTv[:, kc, q * QT:(q + 1) * QT], start=(kc == 0), stop=(kc == 7))
                act(gys[s_], py[:], AF.Gelu_apprx_tanh)
            prs = []
            for s_ in two:
                pr_ = bank()
                pi_ = bank()
                mm(pr_[:], bdv[:, s_, 0, :], xrbs[s_][b])
                mm(pi_[:], bdv[:, s_, 1, :], xrbs[s_][b])
                prs.append((pr_, pi_))
            for s_ in two:
                i = 2 * pr + s_
                act(RAs[s_], prs[s_][0][:], AF.Tanh, bias=dv[:, HBR + i:HBR + i + 1], scale=0.5)
            for s_ in two:
                i = 2 * pr + s_
                act(Ibs[s_], prs[s_][1][:], AF.Tanh, bias=dv[:, HBI + i:HBI + i + 1], scale=0.5)
            for s_ in two:
                i = 2 * pr + s_
                act(MHs[s_], RAs[s_], AF.Exp, bias=dv[:, CL + i:CL + i + 1], scale=dv[:, CL + i:CL + i + 1])
            for s_ in two:
                i = 2 * pr + s_
                act(RAs[s_], RAs[s_], AF.Exp, bias=dv[:, HCL + i:HCL + i + 1], scale=dv[:, HCL + i:HCL + i + 1])
            for s_ in two:
                tsc(MHs[s_], MHs[s_], 0.99999994, None, ALU.min)
            for s_ in two:
                act(MHs[s_], MHs[s_], AF.Sqrt, bias=small[:, 4:5], scale=-0.25)
            for s_ in two:
                stt(Ibs[s_], Ibs[s_], 1.0, XRs[s_][b], ALU.add, ALU.mult)
            for s_ in two:
                tt(Ibs[s_], Ibs[s_], MHs[s_], ALU.mult)
            for s_ in two:
                init = 0.0 if q == 0 else hlasts[s_]
                rd = [RAs[s_], Ibs[s_]] + ([hlasts[s_]] if q > 0 else [])
                S.add("dve", (lambda init_, Ib_, MH_, RA_: (lambda e: e.tensor_tensor_scan(out=MH_, data0=RA_, data1=Ib_, initial=init_,
                                                                                            op0=ALU.mult, op1=ALU.add)))(init, Ibs[s_], MHs[s_], RAs[s_]),
                      reads=rd, writes=[MHs[s_]])
            for s_ in two:
                if q < 3:
                    cp(hlasts[s_], MHs[s_][:, QT - 1:QT])
            for s_ in two:
                i = 2 * pr + s_
                tt(yC[:, i, q * QT:(q + 1) * QT], gys[s_], MHs[s_], ALU.mult)

        for pr in range(2):
            lru_A(4 * pr)
            for q in range(4):
                it = 4 * pr + q
                if q < 3:
                    lru_A(it + 1)
                lru_B(it)
        if dbg and l == 0:
            dbg_ops.append(dma(dbg_d["yC"], arv(33280, 16384)))

        mg = arv(49664, 16384).rearrange("p (c t) -> p c t", c=4)
        brs = [yA, oT, yC]
        for half in range(2):
            for mm_i in range(4):
                m = half * 4 + mm_i
                g2 = wload(wg2_d[l, m], 2048)
                g1 = wload(wg1_d[l, m], 1024)
                g2v = g2[:].rearrange("p (k n) -> p k n", k=8)
                g1v = g1[:, 0:1024].rearrange("p (k n) -> p k n", k=8)
                gw = [g2v[:, :, 0:128], g2v[:, :, 128:256], g1v[:, :, 0:128]]
                pslot = wload(pout_d[l, m], 1536)
                pv4 = pslot[:, 0:1536].rearrange("p (b k n) -> p b k n", b=3, k=4)
                for tb in range(NTB):
                    sgs = []
                    for b in range(3):
                        pg = bank()
                        for kc in range(8):
                            mm(pg[:], gw[b][:, kc, :], hTv[:, kc, tb * TB:(tb + 1) * TB], start=(kc == 0), stop=(kc == 7))
                        sgt = tmp[b][:]
                        act(sgt, pg[:], AF.Sigmoid)
                        sgs.append(sgt)
                    acc1 = tmp[3][:]
                    acc2 = tmp[4][:]
                    for b in range(3):
                        pbr = bank()
                        for kc in range(4):
                            mm(pbr[:], pv4[:, b, kc, :], brs[b][:, kc, tb * TB:(tb + 1) * TB], start=(kc == 0), stop=(kc == 3))
                        if b == 0:
                            tt(acc1, pbr[:], sgs[b], ALU.mult)
                        else:
                            tt(acc2, pbr[:], sgs[b], ALU.mult)
                            if b == 1:
                                tt(acc1, acc1, acc2, ALU.add)
                            else:
                                tt(mg[:, mm_i, tb * TB:(tb + 1) * TB], acc1, acc2, ALU.add)
            if half == 0:
                for g in range(4):
                    slot = wload(wo_d[l, g], 2048)
                    sv = slot[:].rearrange("p (k n) -> p k n", k=8)
                    for jj in range(2):
                        mo = 2 * g + jj
                        for tb in range(NTB):
                            po = bank()
                            for kc in range(4):
                                mm(po[:], sv[:, half * 4 + kc, jj * 128:(jj + 1) * 128], mg[:, kc, tb * TB:(tb + 1) * TB], start=(kc == 0), stop=(kc == 3))
                            xs = xTv[:, mo, tb * TB:(tb + 1) * TB]
                            stt(xs, po[:], dv[:, MOD + 16 + mo:MOD + 17 + mo], xs, ALU.mult, ALU.add)
            else:
                default_pool[0] = tuple(range(7))
                svs4 = [wload(wo_d[l, g], 2048)[:].rearrange("p (k n) -> p k n", k=8) for g in range(4)]
                for tb in range(NTB):
                    for mo in range(8):
                        po = bank()
                        for kc in range(4):
                            mm(po[:], svs4[mo // 2][:, half * 4 + kc, (mo % 2) * 128:(mo % 2) * 128 + 128], mg[:, kc, tb * TB:(tb + 1) * TB], start=(kc == 0), stop=(kc == 3))
                        xs = xTv[:, mo, tb * TB:(tb + 1) * TB]
                        stt(xs, po[:], dv[:, MOD + 16 + mo:MOD + 17 + mo], xs, ALU.mult, ALU.add)
                    norm_stats(tb)
                    if tb > 0:
                        norm_apply(tb - 1, dv, A2, MOD + 24)
                norm_apply(NTB - 1, dv, A2, MOD + 24)
        if dbg and l == 0:
            dbg_ops.append(dma(dbg_d["xmid"], xT[:]))

        default_pool[0] = tuple(range(7))
        nxt = (l + 1 < NL)
        ada_next = list(range(24)) if nxt else []
        pvt_n, dv_n = pvts[(l + 1) % 2], dvs[(l + 1) % 2]
        if nxt:
            dma(pvt_n[:], pv_d[l + 1])
        hid = arv(0, 45056).rearrange("p (j t) -> p j t", j=22)
        for hf in range(2):
            for j in range(22):
                slot = wload(wfi_d[l, j], 2048)
                sv = slot[:].rearrange("p (k n) -> p k n", k=8)
                for t2 in range(2):
                    tb = hf * 2 + t2
                    pa = bank()
                    pb = bank()
                    for kc in range(8):
                        mm(pa[:], sv[:, kc, 0:128], hTv[:, kc, tb * TB:(tb + 1) * TB], start=(kc == 0), stop=(kc == 7))
                    for kc in range(8):
                        mm(pb[:], sv[:, kc, 128:256], hTv[:, kc, tb * TB:(tb + 1) * TB], start=(kc == 0), stop=(kc == 7))
                    sa = tmp[(2 * j + t2) % 4][:]
                    act(sa, pa[:], AF.Silu)
                    tt(hid[:, j, t2 * TB:(t2 + 1) * TB], pb[:], sa, ALU.mult)
                if hf == 0 and ada_next:
                    ada_group(l + 1, ada_next.pop(0))
                if hf == 1 and nxt:
                    if j == 1:
                        norm_stats(0, scr_ffn)
                    elif j == 3:
                        norm_stats(1, scr_ffn)
                    elif j == 5:
                        norm_apply(0, dv_n, A1, MOD + 0, scr_ffn)
                    elif j == 8:
                        norm_apply(1, dv_n, A1, MOD + 0, scr_ffn)
            if hf == 0:
                for mo in range(8):
                    sl0 = wload(wfo_d[l, 2 * mo], 1408)
                    sl1 = wload(wfo_d[l, 2 * mo + 1], 1408)
                    svs = [s_[:, 0:1408].rearrange("p (k n) -> p k n", k=11) for s_ in (sl0, sl1)]
                    for t2 in range(2):
                        tb = t2
                        po = bank()
                        for j in range(22):
                            mm(po[:], svs[j // 11][:, j % 11, :], hid[:, j, t2 * TB:(t2 + 1) * TB], start=(j == 0), stop=(j == 21))
                        xs = xTv[:, mo, tb * TB:(tb + 1) * TB]
                        stt(xs, po[:], dv[:, MOD + 40 + mo:MOD + 41 + mo], xs, ALU.mult, ALU.add)
                    if ada_next:
                        ada_group(l + 1, ada_next.pop(0))
                if nxt:
                    assert not ada_next
                    finalize_params(pvt_n, dv_n)
            else:
                for t2 in range(2):
                    tb = 2 + t2
                    for mo in range(8):
                        sl0 = wload(wfo_d[l, 2 * mo], 1408)
                        sl1 = wload(wfo_d[l, 2 * mo + 1], 1408)
                        svs = [s_[:, 0:1408].rearrange("p (k n) -> p k n", k=11) for s_ in (sl0, sl1)]
                        po = bank()
                        for j in range(22):
                            mm(po[:], svs[j // 11][:, j % 11, :], hid[:, j, t2 * TB:(t2 + 1) * TB], start=(j == 0), stop=(j == 21))
                        xs = xTv[:, mo, tb * TB:(tb + 1) * TB]
                        stt(xs, po[:], dv[:, MOD + 40 + mo:MOD + 41 + mo], xs, ALU.mult, ALU.add)
                        if nxt and t2 == 1 and mo == 1:
                            norm_stats(2, scr_ffn)
                        if nxt and t2 == 1 and mo == 4:
                            norm_apply(2, dv_n, A1, MOD + 0, scr_ffn)
                if nxt:
                    norm_stats(3, scr_ffn)
                    norm_apply(3, dv_n, A1, MOD + 0, scr_ffn)

    outs = []
    for tt_i in range(16):
        xio = xios[tt_i % 4]
        for half in range(2):
            pb = bank()
            for q in range(4):
                kc = half * 4 + q
                tr(pb[:, q * 128:(q + 1) * 128], xTv[:, kc, tt_i * 128:(tt_i + 1) * 128], identf[:])
            if half == 0:
                cp(xio[:, 0:512], pb[:], eng="dve")
            else:
                cp(xio[:, 512:1024], pb[:], eng="act")
        outs.append(dma(out_d[tt_i * 128:(tt_i + 1) * 128, :], xio))
    S.emit(final_wait_ops=outs + dbg_ops)
    es.close()
    return nc


def _win_perm():
    order = []
    for c in range(4):
        order += [1024 + 128 * c, 1536 + 128 * c]
    for c in range(4):
        order += [2048 + 128 * c]
    for i in range(4):
        order += [512 + 128 * i, 128 * i]
    for i in range(4):
        order += [3080 + 128 * i, 2568 + 128 * i]
    for m in range(8):
        order += [3592 + 128 * m, 4616 + 128 * m, 5640 + 128 * m]
    cols = np.concatenate([np.arange(s, s + 128) for s in order])
    return cols


def _kmajor(w, ncols_per_group):
    L, K, C = w.shape
    kc = K // 128
    g = ncols_per_group
    a = w.reshape(L, kc, 128, C // g, g)
    a = a.transpose(0, 3, 2, 1, 4)
    return np.ascontiguousarray(a.reshape(L, C // g, 128, kc * g))


def _chunkvec(v):
    L, n = v.shape
    return v.reshape(L, n // 128, 128).transpose(0, 2, 1)


def prep_inputs(inp, NL=NLAYERS):
    f = lambda a: np.asarray(a, dtype=np.float32)[:NL]
    shared = {}
    shared["wada"] = _kmajor(f(inp["w_ada"]), 256)
    perm = _win_perm()
    win = f(inp["w_in"])
    shared["win"] = _kmajor(np.ascontiguousarray(win[:, :, perm[:28 * 128]]), 256)
    g2cols = np.concatenate([np.concatenate([np.arange(3592 + 128 * m, 3592 + 128 * m + 128), np.arange(4616 + 128 * m, 4616 + 128 * m + 128)]) for m in range(8)])
    g1cols = np.concatenate([np.arange(5640 + 128 * m, 5640 + 128 * m + 128) for m in range(8)])
    shared["wg2"] = _kmajor(np.ascontiguousarray(win[:, :, g2cols]), 256)
    shared["wg1"] = _kmajor(np.ascontiguousarray(win[:, :, g1cols]), 128)
    wfc = win[:, :, 2560:2568]
    wf = np.concatenate([wfc, wfc], axis=2)
    shared["wf"] = np.ascontiguousarray(wf.reshape(NL, 8, 128, 16).transpose(0, 2, 1, 3).reshape(NL, 128, 128))
    bd = np.zeros((NL, 128, 8, 128), np.float32)
    wr = f(inp["lru_w_r"])
    wi = f(inp["lru_w_i"])
    for i in range(4):
        for g, w in enumerate((wr, wi)):
            bd[:, 0:64, 2 * i + g, 0:64] = w[:, 2 * i]
            bd[:, 64:128, 2 * i + g, 64:128] = w[:, 2 * i + 1]
    shared["bd"] = bd.reshape(NL, 128, 1024)
    pA = _kmajor(f(inp["w_conv_out"]), 128)
    pB = _kmajor(f(inp["w_fox_out"]), 128)
    pC = _kmajor(f(inp["w_lru_out"]), 128)
    shared["pout"] = np.ascontiguousarray(np.concatenate([pA, pB, pC], axis=3))
    shared["wo"] = _kmajor(f(inp["w_o"]), 256)
    wfi = f(inp["w_ffn_in"])
    fperm = np.concatenate([np.concatenate([np.arange(128 * j, 128 * j + 128), np.arange(2816 + 128 * j, 2816 + 128 * j + 128)]) for j in range(22)])
    shared["wfi"] = _kmajor(np.ascontiguousarray(wfi[:, :, fperm]), 256)
    wfo = f(inp["w_ffn_out"])
    a = wfo.reshape(NL, 2, 11, 128, 8, 128)
    a = a.transpose(0, 4, 1, 3, 2, 5)
    shared["wfo"] = np.ascontiguousarray(a.reshape(NL, 16, 128, 1408))
    pv = np.zeros((NL, 128, NPV), np.float32)
    pv[:, :, BADA:BADA + 48] = _chunkvec(f(inp["b_ada"]))
    pv[:, :, GN1:GN1 + 8] = _chunkvec(f(inp["g_norm_mix"]))
    pv[:, :, GN2:GN2 + 8] = _chunkvec(f(inp["g_norm_ffn"]))
    cw = f(inp["conv_dw_w"])
    pv[:, :, CW:CW + 124] = cw.reshape(NL, 31, 4, 128).transpose(0, 3, 2, 1).reshape(NL, 128, 124)
    pv[:, :, CB:CB + 4] = _chunkvec(f(inp["conv_dw_b"]))
    pv[:, :, CLG:CLG + 4] = _chunkvec(f(inp["conv_ln_g"]))
    pv[:, :, CLB:CLB + 4] = _chunkvec(f(inp["conv_ln_b"]))
    pv[:, :, GQ] = np.tile(f(inp["fox_q_norm_g"]), (1, 2))
    pv[:, :, GK] = np.tile(f(inp["fox_k_norm_g"]), (1, 2))
    pv[:, :, BF_] = np.tile(f(inp["fox_b_f"]), (1, 16))
    lw = f(inp["lru_conv_w"])
    pv[:, :, LW:LW + 16] = lw.reshape(NL, 4, 4, 128).transpose(0, 3, 2, 1).reshape(NL, 128, 16)
    pv[:, :, LB:LB + 4] = _chunkvec(f(inp["lru_conv_b"]))
    pv[:, :, LBR:LBR + 4] = _chunkvec(f(inp["lru_b_r"]))
    pv[:, :, LBI:LBI + 4] = _chunkvec(f(inp["lru_b_i"]))
    pv[:, :, LAM:LAM + 4] = _chunkvec(f(inp["lru_lambda"]))
    shared["pv"] = pv
    return shared


_NC_CACHE = {}


def kernel(**inputs):
    x = np.asarray(inputs["x"], dtype=np.float32)
    c = np.asarray(inputs["c"], dtype=np.float32)
    B = x.shape[0]
    shared = prep_inputs(inputs, NLAYERS)
    if "nc" not in _NC_CACHE:
        _NC_CACHE["nc"] = build(NLAYERS)
    nc = _NC_CACHE["nc"]
    in_maps = []
    for b in range(B):
        m = dict(shared)
        m["x"] = np.ascontiguousarray(x[b])
        m["c"] = np.ascontiguousarray(c[b].reshape(8, 128).T)
        in_maps.append(m)
    res = run_bass_kernel_spmd(nc, in_maps, core_ids=list(range(B)))
    return np.stack([np.asarray(r["out"], dtype=np.float32) for r in res.results], axis=0)
```

```python
import numpy as np
from contextlib import ExitStack
import concourse.bass as bass
import concourse.mybir as mybir
from concourse.bass_utils import run_bass_kernel_spmd

F32 = mybir.dt.float32
BF16 = mybir.dt.bfloat16
AF = mybir.ActivationFunctionType
ALU = mybir.AluOpType
DT_SIZE = {mybir.dt.float32: 4, mybir.dt.bfloat16: 2}

T = 2048
D = 1024
TB = 512
NTB = 4
NLAYERS = 4
EPS = 1e-6
PADV = 32
ENGS = ("pe", "act", "dve", "pool", "sp")

BADA, GN1, GN2, CW, CB, CLG, CLB, GQ, GK, BF_, LW, LB, LBR, LBI, LAM, NPV = (
    0, 48, 56, 64, 188, 192, 196, 200, 201, 202, 203, 219, 223, 227, 231, 236)


class Op:
    __slots__ = ("eng", "fn", "deps", "is_dma", "sig", "semval", "dsem", "dval", "prev_dma")

    def __init__(self, eng, fn, is_dma):
        self.eng = eng
        self.fn = fn
        self.deps = set()
        self.is_dma = is_dma
        self.sig = False
        self.semval = 0
        self.dsem = None
        self.dval = 0
        self.prev_dma = None


class Sched:
    def __init__(self, nc, n_dma_sems=16):
        self.nc = nc
        self.streams = {e: [] for e in ENGS}
        self.lastw = {}
        self.readers = {}
        self.n_dma_sems = n_dma_sems
        self.dma_count = 0
        self.dma_sem_last = [None] * n_dma_sems
        self.dma_sem_uses = [0] * n_dma_sems

    @staticmethod
    def blocks(ap):
        space = str(ap.space)
        if space not in ("SB", "PSUM"):
            return ()
        dims = ap.ap
        esz = DT_SIZE[ap.dtype]
        pstride = dims[0][0]
        off = ap.offset
        col = off % pstride if pstride > 0 else off
        ext = 0
        for st, cnt in dims[1:]:
            if cnt > 1:
                ext += (cnt - 1) * abs(st)
        lo = col * esz
        hi = (col + ext + 1) * esz
        bs = 512 if space == "SB" else 256
        name = ap.tensor.name
        return [(name, b) for b in range(lo // bs, (hi - 1) // bs + 1)]

    def add(self, eng, fn, reads=(), writes=(), dma=False):
        op = Op(eng, fn, dma)
        deps = op.deps
        for ap in reads:
            for k in self.blocks(ap):
                w = self.lastw.get(k)
                if w is not None:
                    deps.add(w)
                self.readers.setdefault(k, []).append(op)
        for ap in writes:
            for k in self.blocks(ap):
                w = self.lastw.get(k)
                if w is not None:
                    deps.add(w)
                for r in self.readers.get(k, ()):
                    deps.add(r)
                self.lastw[k] = op
                self.readers[k] = []
        deps.discard(op)
        if dma:
            s = self.dma_count % self.n_dma_sems
            self.dma_count += 1
            op.prev_dma = self.dma_sem_last[s]
            self.dma_sem_last[s] = op
            self.dma_sem_uses[s] += 1
            op.dsem = s
            op.dval = 16 * self.dma_sem_uses[s]
        self.streams[eng].append(op)
        return op

    def emit(self, final_wait_ops=()):
        nc = self.nc
        for e in ENGS:
            for op in self.streams[e]:
                for d in op.deps:
                    if d.is_dma:
                        continue
                    if d.eng == "pe" and op.eng == "pe" and not op.is_dma:
                        continue
                    d.sig = True
        for e in ENGS:
            c = 0
            for op in self.streams[e]:
                if not op.is_dma and op.sig:
                    c += 1
                    op.semval = c
        with ExitStack() as es:
            esem = {e: es.enter_context(nc.semaphore("s_" + e)) for e in ("pe", "act", "dve", "pool", "sp")}
            dsems = [es.enter_context(nc.semaphore("d%d" % i)) for i in range(self.n_dma_sems)]
            block = es.enter_context(nc.Block())

            def run(ename, eng):
                seen = {}
                for op in self.streams[ename]:
                    need = {}
                    for d in op.deps:
                        if d.is_dma:
                            k = ("d", d.dsem)
                            need[k] = max(need.get(k, 0), d.dval)
                        else:
                            if d.eng == "pe" and ename == "pe" and not op.is_dma:
                                continue
                            k = ("e", d.eng)
                            need[k] = max(need.get(k, 0), d.semval)
                    if op.is_dma and op.prev_dma is not None:
                        k = ("d", op.dsem)
                        need[k] = max(need.get(k, 0), op.prev_dma.dval)
                    for k, v in need.items():
                        if seen.get(k, 0) >= v:
                            continue
                        seen[k] = v
                        eng.wait_ge(dsems[k[1]] if k[0] == "d" else esem[k[1]], v)
                    ins = op.fn(eng)
                    if op.is_dma:
                        ins.then_inc(dsems[op.dsem], 16)
                    elif op.sig:
                        ins.then_inc(esem[ename], 1)
                if ename == "sp":
                    for op in final_wait_ops:
                        eng.wait_ge(dsems[op.dsem], op.dval)

            @block.tensor
            def _(e):
                run("pe", e)

            @block.scalar
            def _(e):
                run("act", e)

            @block.vector
            def _(e):
                run("dve", e)

            @block.gpsimd
            def _(e):
                run("pool", e)

            @block.sync
            def _(e):
                run("sp", e)


def build(NL=NLAYERS, dbg=False):
    nc = bass.Bass("TRN2", target_bir_lowering=False)

    def dram(name, shape, dt=F32, kind="ExternalInput"):
        return nc.dram_tensor(name, shape, dt, kind=kind).ap()

    x_d = dram("x", [T, D])
    c_d = dram("c", [128, 8])
    wada_d = dram("wada", [NL, 24, 128, 2048])
    win_d = dram("win", [NL, 14, 128, 2048])
    wg2_d = dram("wg2", [NL, 8, 128, 2048])
    wg1_d = dram("wg1", [NL, 8, 128, 1024])
    wf_d = dram("wf", [NL, 128, 128])
    bd_d = dram("bd", [NL, 128, 1024])
    pout_d = dram("pout", [NL, 8, 128, 1536])
    wo_d = dram("wo", [NL, 4, 128, 2048])
    wfi_d = dram("wfi", [NL, 22, 128, 2048])
    wfo_d = dram("wfo", [NL, 16, 128, 1408])
    pv_d = dram("pv", [NL, 128, NPV])
    out_d = dram("out", [T, D], kind="ExternalOutput")
    dbg_d = {}
    dbg_ops = []
    if dbg:
        for nm in ("oT", "yA", "yC"):
            dbg_d[nm] = dram("dbg_" + nm, [128, 4 * T], BF16, kind="ExternalOutput")
        dbg_d["xmid"] = dram("dbg_xmid", [128, 8 * T], F32, kind="ExternalOutput")
        dbg_d["hT"] = dram("dbg_hT", [128, 8 * T], BF16, kind="ExternalOutput")

    es = ExitStack()

    def sb(name, shape, dt):
        return es.enter_context(nc.sbuf_tensor(name, shape, dt))

    xT = sb("xT", [128, 8 * T], F32)
    hT = sb("hT", [128, 8 * T], BF16)
    ARENA = 74240
    ar = sb("arena", [128, ARENA // 2], BF16)
    ws = [sb("ws%d" % i, [128, 2048], BF16) for i in range(4)]
    tmpbig = sb("tmpbig", [128, 6 * 512], F32)

    class _Sl:
        def __init__(self, ap):
            self.ap = ap

        def __getitem__(self, idx):
            return self.ap if idx == slice(None) else self.ap[idx]

    tmp = [_Sl(tmpbig[:, i * 512:(i + 1) * 512]) for i in range(6)]
    identf = sb("identf", [128, 128], F32)
    identb = sb("identb", [128, 128], BF16)
    onesb = sb("onesb", [128, 128], BF16)
    blockones = sb("blockones", [128, 128], BF16)
    maskb = sb("maskb", [128, 128], BF16)
    negmask = sb("negmask", [128, 128], BF16)
    selb = sb("selb", [16, 8 * 128], BF16)
    zerob = sb("zerob", [128, 272], BF16)
    pvts = [sb("pvt%d" % i, [128, NPV], F32) for i in range(2)]
    dvs = [sb("dv%d" % i, [128, 128], F32) for i in range(2)]
    cact = sb("cact", [128, 8], BF16)
    cf = sb("cf", [128, 8], F32)
    small = sb("small", [128, 64], F32)
    rcs = sb("rcs", [128, 8], F32)
    otoks = [sb("otok%d" % i, [128, 512], BF16) for i in range(2)]
    lrs = sb("lrs", [128, 8], F32)
    negcumT = sb("negcumT", [128, 128], F32)
    ps = [es.enter_context(nc.psum_tensor("ps%d" % i, [128, 512], F32)) for i in range(8)]

    S = Sched(nc)

    def arv(off, nbytes, dt=BF16):
        a = ar[:, off // 2:(off + nbytes) // 2]
        if dt == F32:
            a = a.bitcast(F32)
        return a

    def isap(v):
        return not isinstance(v, (int, float)) and v is not None

    def mm(out, lhsT, rhs, start=True, stop=True, skip=False):
        return S.add("pe", lambda e: e.matmul(out, lhsT=lhsT, rhs=rhs, start=start, stop=stop,
                                              skip_group_check=skip), reads=[lhsT, rhs], writes=[out])

    def tr(out, in_, ident):
        return S.add("pe", lambda e: e.transpose(out=out, in_=in_, identity=ident), reads=[in_, ident], writes=[out])

    def act(out, in_, func, bias=None, scale=None):
        kw = {}
        rd = [in_]
        if bias is not None:
            kw["bias"] = bias
            if isap(bias):
                rd.append(bias)
        if scale is not None:
            kw["scale"] = scale
            if isap(scale):
                rd.append(scale)
        return S.add("act", lambda e: e.activation(out=out, in_=in_, func=func, **kw), reads=rd, writes=[out])

    def tt(out, in0, in1, op, eng="dve"):
        return S.add(eng, lambda e: e.tensor_tensor(out=out, in0=in0, in1=in1, op=op), reads=[in0, in1], writes=[out])

    def tsc(out, in0, s1, s2, op0, op1=None, eng="dve"):
        rd = [in0] + [s for s in (s1, s2) if isap(s)]
        if op1 is None:
            return S.add(eng, lambda e: e.tensor_scalar(out=out, in0=in0, scalar1=s1, scalar2=None, op0=op0), reads=rd, writes=[out])
        return S.add(eng, lambda e: e.tensor_scalar(out=out, in0=in0, scalar1=s1, scalar2=s2, op0=op0, op1=op1), reads=rd, writes=[out])

    def stt(out, in0, scalar, in1, op0, op1):
        rd = [in0, in1] + ([scalar] if isap(scalar) else [])
        return S.add("dve", lambda e: e.scalar_tensor_tensor(out=out, in0=in0, scalar=scalar, in1=in1, op0=op0, op1=op1), reads=rd, writes=[out])

    def cp(out, in_, eng="dve"):
        if eng == "act":
            return S.add("act", lambda e: e.copy(out=out, in_=in_), reads=[in_], writes=[out])
        return S.add(eng, lambda e: e.tensor_copy(out=out, in_=in_), reads=[in_], writes=[out])

    def recip(out, in_):
        return S.add("dve", lambda e: e.reciprocal(out=out, in_=in_), reads=[in_], writes=[out])

    def mset(ap, val, eng="pool"):
        return S.add(eng, lambda e: e.memset(ap, val), writes=[ap])

    def dma(out, in_, eng="sp", **kw):
        return S.add(eng, lambda e: e.dma_start(out=out, in_=in_, **kw), reads=[in_], writes=[out], dma=True)

    wctr = [0]

    def wload(src, n):
        slot = ws[wctr[0] % 4]
        wctr[0] += 1
        dst = slot[:, 0:n]
        S.add("pool", lambda e: e.dma_start(out=dst, in_=src, max_dma_last_dim=4096), reads=[], writes=[dst], dma=True)
        return slot

    pctr = {}
    default_pool = [tuple(range(7))]

    def bank(pool=None):
        pool = default_pool[0] if pool is None else tuple(pool)
        n = pctr.get(pool, 0)
        pctr[pool] = n + 1
        return ps[pool[n % len(pool)]]

    mset(identf[:], 0.0)
    S.add("pool", lambda e: e.affine_select(out=identf[:], in_=identf[:], pattern=[[-1, 128]], compare_op=ALU.not_equal,
                                            fill=1.0, base=0, channel_multiplier=1), reads=[identf[:]], writes=[identf[:]])
    cp(identb[:], identf[:], eng="pool")
    mset(onesb[:], 1.0)
    mset(zerob[:], 0.0)
    mset(blockones[:], 0.0)
    mset(blockones[0:64, 0:64], 1.0)
    mset(blockones[64:128, 64:128], 1.0)
    S.add("pool", lambda e: e.affine_select(out=maskb[:], in_=onesb[:], pattern=[[1, 128]], compare_op=ALU.is_ge,
                                            fill=0.0, base=0, channel_multiplier=-1), reads=[onesb[:]], writes=[maskb[:]])
    tsc(negmask[:], maskb[:], -1.0, 30000.0, ALU.add, ALU.mult, eng="pool")
    mset(selb[:], 0.0)
    selv = selb[:].rearrange("p (h m) -> p h m", h=8)
    S.add("pool", lambda e: e.affine_select(out=selv, in_=selv, pattern=[[-1, 8], [0, 128]], compare_op=ALU.not_equal,
                                            fill=1.0, base=0, channel_multiplier=1), reads=[selb[:]], writes=[selb[:]])
    S.add("pool", lambda e: e.affine_select(out=selv, in_=selv, pattern=[[-1, 8], [0, 128]], compare_op=ALU.not_equal,
                                            fill=1.0, base=-8, channel_multiplier=1), reads=[selb[:]], writes=[selb[:]])
    mset(small[:, 0:1], 1.0)
    mset(small[:, 1:2], EPS)
    mset(small[:, 4:5], 0.25)
    mset(small[:, 2:4], 0.0)
    mset(small[0:8, 2:3], 1.0)
    mset(small[:, 3:4], 1.0)
    mset(small[0:8, 3:4], 0.0)
    ones_col = small[:, 0:1]
    eps_col = small[:, 1:2]

    xTv = xT[:].rearrange("p (c t) -> p c t", c=8)
    hTv = hT[:].rearrange("p (c t) -> p c t", c=8)
    xios = [ar[:, i * 2048:(i + 1) * 2048].bitcast(F32) for i in range(4)]
    for tt_i in range(16):
        xio = xios[tt_i % 4]
        dma(xio, x_d[tt_i * 128:(tt_i + 1) * 128, :])
        for half in range(2):
            pb = bank()
            for q in range(4):
                kc = half * 4 + q
                tr(pb[:, q * 128:(q + 1) * 128], xio[:, kc * 128:(kc + 1) * 128], identf[:])
            dst = xTv[:, half * 4:half * 4 + 4, tt_i * 128:(tt_i + 1) * 128]
            src = pb[:].rearrange("p (q t) -> p q t", q=4)
            if half == 0:
                cp(dst, src, eng="dve")
            else:
                cp(dst, src, eng="act")

    dma(cf[:], c_d)
    act(cact[:], cf[:], AF.Silu)

    scr_tmp = ([tmp[0][:], tmp[1][:]],
               [tmp[5][:].bitcast(BF16)[:, 0:512], tmp[5][:].bitcast(BF16)[:, 512:1024]],
               [tmp[2][:], tmp[3][:], tmp[4][:]])
    scr_ffn = ([arv(45056, 2048, F32), arv(47104, 2048, F32)],
               [arv(49152, 1024), arv(50176, 1024)],
               [arv(51200, 2048, F32), arv(53248, 2048, F32), arv(55296, 2048, F32)])

    def norm_stats(tb, scr=None):
        rstds, sqs, _ = scr or scr_tmp
        pb = bank()
        for kc in range(8):
            s_ = sqs[kc % 2]
            xs = xTv[:, kc, tb * TB:(tb + 1) * TB]
            if kc % 2 == 0:
                act(s_, xs, AF.Square)
            else:
                tt(s_, xs, xs, ALU.mult)
            mm(pb[:], onesb[:], s_, start=(kc == 0), stop=(kc == 7))
        act(rstds[tb % 2], pb[:], AF.Ln, bias=eps_col, scale=1.0 / D)
        act(rstds[tb % 2], rstds[tb % 2], AF.Exp, scale=-0.5)

    def norm_apply(tb, dvt, Acol, Bcol, scr=None):
        rstds, _, xns = scr or scr_tmp
        for kc in range(8):
            xn = xns[kc % 3]
            tt(xn, xTv[:, kc, tb * TB:(tb + 1) * TB], rstds[tb % 2], ALU.mult)
            if kc % 4 == 3:
                tsc(hTv[:, kc, tb * TB:(tb + 1) * TB], xn, dvt[:, Acol + kc:Acol + kc + 1], dvt[:, Bcol + kc:Bcol + kc + 1], ALU.mult, ALU.add)
            else:
                act(hTv[:, kc, tb * TB:(tb + 1) * TB], xn, AF.Identity, bias=dvt[:, Bcol + kc:Bcol + kc + 1],
                    scale=dvt[:, Acol + kc:Acol + kc + 1])

    def norm_to_hT(dvt, Acol, Bcol):
        norm_stats(0)
        for tb in range(NTB):
            if tb + 1 < NTB:
                norm_stats(tb + 1)
            norm_apply(tb, dvt, Acol, Bcol)

    MOD = 0
    A1 = 48
    A2 = 56
    GQ8 = 64
    NEGBF = 65
    CL = 66
    CL2 = 70
    SCR = 80
    HBR = 100
    HBI = 104
    HCL = 108

    def ada_group(l_, g):
        slot = wload(wada_d[l_, g], 2048)
        sv = slot[:].rearrange("p (k n) -> p k n", k=8)
        for jj in range(2):
            j = 2 * g + jj
            for kc in range(8):
                mm(ps[7][:, j:j + 1], sv[:, kc, jj * 128:(jj + 1) * 128], cact[:, kc:kc + 1], start=(kc == 0), stop=(kc == 7), skip=True)

    def finalize_params(pvt, dv, part=0):
        pm = ps[7]
        if part in (0, 1):
            tt(dv[:, MOD:MOD + 16], pm[:, 0:16], pvt[:, BADA:BADA + 16], ALU.add)
            stt(dv[:, A1:A1 + 8], dv[:, MOD + 8:MOD + 16], 1.0, pvt[:, GN1:GN1 + 8], ALU.add, ALU.mult)
            if part == 1:
                return
        tt(dv[:, MOD + 16:MOD + 48], pm[:, 16:48], pvt[:, BADA + 16:BADA + 48], ALU.add)
        stt(dv[:, A2:A2 + 8], dv[:, MOD + 32:MOD + 40], 1.0, pvt[:, GN2:GN2 + 8], ALU.add, ALU.mult)
        tsc(dv[:, GQ8:GQ8 + 1], pvt[:, GQ:GQ + 1], 0.125, None, ALU.mult)
        tsc(dv[:, NEGBF:NEGBF + 1], pvt[:, BF_:BF_ + 1], -1.0, None, ALU.mult)
        e_ = dv[:, SCR:SCR + 4]
        w_ = dv[:, SCR + 4:SCR + 8]
        d_ = dv[:, SCR + 8:SCR + 12]
        l_ = dv[:, SCR + 12:SCR + 16]
        act(e_, pvt[:, LAM:LAM + 4], AF.Exp, scale=-1.0)
        tsc(w_, e_, 1.0, None, ALU.add)
        tsc(d_, w_, -1.0, None, ALU.add)
        recip(d_, d_)
        act(l_, w_, AF.Ln)
        tt(l_, l_, e_, ALU.mult)
        tt(l_, l_, d_, ALU.mult)
        tsc(dv[:, CL:CL + 4], l_, -8.0, None, ALU.mult)
        tsc(dv[:, CL2:CL2 + 4], l_, -16.0, None, ALU.mult)
        tsc(dv[:, HCL:HCL + 4], l_, -4.0, None, ALU.mult)
        tsc(dv[:, HBR:HBR + 4], pvt[:, LBR:LBR + 4], 0.5, None, ALU.mult)
        tsc(dv[:, HBI:HBI + 4], pvt[:, LBI:LBI + 4], 0.5, None, ALU.mult)

    for l in range(NL):
        pvt = pvts[l % 2]
        dv = dvs[l % 2]
        if l == 0:
            dma(pvt[:], pv_d[0])
            for g in range(8):
                ada_group(0, g)
            finalize_params(pvt, dv, part=1)
            rest = list(range(8, 24))
            norm_stats(0)
            for tb in range(NTB):
                for _ in range(2):
                    ada_group(0, rest.pop(0))
                if tb + 1 < NTB:
                    norm_stats(tb + 1)
                norm_apply(tb, dv, A1, MOD + 0)
                for _ in range(2):
                    ada_group(0, rest.pop(0))
            assert not rest
            finalize_params(pvt, dv, part=2)
            default_pool[0] = tuple(range(8))
        else:
            default_pool[0] = tuple(range(8))
        if dbg and l == 0:
            dbg_ops.append(dma(dbg_d["hT"], hT[:]))

        oT = arv(0, 16384).rearrange("p (c t) -> p c t", c=4)
        Vaug = arv(16384, 16640).rearrange("p (j h d) -> p j h d", j=16, h=8)
        qkset = [[arv(base + s_ * 4096, 4096) for s_ in range(4)] for base in (33280, 57856)]
        cumHL = arv(49664, 4096)
        PT = [arv(53760 + i * 1024, 1024) for i in range(4)]
        hi_bf = arv(53760, 4096)
        bufA = arv(57856, 8192, F32)
        bufB = arv(66048, 8192, F32)

        mset(Vaug[:, :, :, 64:65], 1.0, eng="dve")
        wfslot = wload(wf_d[l], 128)
        wfv = wfslot[:, 0:128].rearrange("p (k n) -> p k n", k=8)
        for tb in range(NTB):
            pb = bank()
            for kc in range(8):
                mm(pb[0:16, :], wfv[:, kc, :], hTv[:, kc, tb * TB:(tb + 1) * TB], start=(kc == 0), stop=(kc == 7))
            act(bufA[0:16, tb * TB:(tb + 1) * TB], pb[0:16, :], AF.Exp, bias=dv[0:16, NEGBF:NEGBF + 1], scale=-1.0)
        act(bufA[0:16, :], bufA[0:16, :], AF.Ln, bias=ones_col[0:16, :], scale=1.0)
        S.add("dve", lambda e: e.tensor_tensor_scan(out=bufB[0:16, :], data0=ones_col[0:16, :].to_broadcast([16, T]),
                                                    data1=bufA[0:16, :], initial=0.0, op0=ALU.mult, op1=ALU.add),
              reads=[bufA[0:16, :], ones_col[0:16, :]], writes=[bufB[0:16, :]])
        pb = bank()
        for j in range(16):
            tr(pb[:, j * 8:(j + 1) * 8], bufB[0:8, j * 128:(j + 1) * 128], identf[0:8, 0:8])
        cp(negcumT[:], pb[:, 0:128])
        tsc(bufA[0:16, :], bufB[0:16, :], -1.0, None, ALU.mult)
        cp(hi_bf[0:16, :], bufA[0:16, :])
        tt(bufB[0:16, :], bufA[0:16, :], hi_bf[0:16, :], ALU.subtract)
        tsc(bufA[0:16, :], hi_bf[0:16, :], small[0:16, 2:3], None, ALU.mult)
        stt(cumHL[0:16, :], bufB[0:16, :], small[0:16, 3:4], bufA[0:16, :], ALU.mult, ALU.add)
        for st in qkset:
            qzA, qzB, kzA, kzB = st
            mset(qzA[64:128, :].bitcast(F32), 0.0, eng="dve")
            mset(kzA[64:128, :].bitcast(F32), 0.0, eng="dve")
            mset(qzB[0:64, :].bitcast(F32), 0.0, eng="dve")
            mset(kzB[0:64, :].bitcast(F32), 0.0, eng="dve")
            cp(qzA[64:80, :], cumHL[0:16, :])
            cp(qzB[0:16, :], cumHL[0:16, :])

        def qk_proj(c):
            slot = wload(win_d[l, c], 2048)
            sv = slot[:].rearrange("p (k n) -> p k n", k=8)
            qzA, qzB, kzA, kzB = qkset[c % 2]
            cp(kzA[64:80, :].rearrange("p (j m) -> p j m", j=16),
               selb[0:16, (2 * c) * 128:(2 * c + 1) * 128].unsqueeze(1).to_broadcast([16, 16, 128]))
            cp(kzB[0:16, :].rearrange("p (j m) -> p j m", j=16),
               selb[0:16, (2 * c + 1) * 128:(2 * c + 2) * 128].unsqueeze(1).to_broadcast([16, 16, 128]))
            for s_i, (dA, dB, gcol) in enumerate(((qzA, qzB, dv[:, GQ8:GQ8 + 1]), (kzA, kzB, pvt[:, GK:GK + 1]))):
                for tb in range(NTB):
                    pb = bank((0, 1, 2, 3, 4))
                    for kc in range(8):
                        mm(pb[:], sv[:, kc, s_i * 128:(s_i + 1) * 128], hTv[:, kc, tb * TB:(tb + 1) * TB], start=(kc == 0), stop=(kc == 7))
                    sq = tmp[5][:].bitcast(BF16)[:, (tb % 2) * 512:(tb % 2) * 512 + 512]
                    act(sq, pb[:], AF.Square)
                    pb2 = bank((7,))
                    mm(pb2[:], blockones[:], sq)
                    rstd = tmp[tb % 4][:]
                    act(rstd, pb2[:], AF.Ln, bias=eps_col, scale=1.0 / 64.0)
                    act(rstd, rstd, AF.Exp, scale=-0.5)
                    stt(dA[0:64, tb * TB:(tb + 1) * TB], pb[0:64, :], gcol[0:64, :], rstd[0:64, :], ALU.mult, ALU.mult)
                    stt(dB[64:128, tb * TB:(tb + 1) * TB], pb[64:128, :], gcol[64:128, :], rstd[64:128, :], ALU.mult, ALU.mult)

        qk_proj(0)

        vs = [wload(win_d[l, 4], 2048), wload(win_d[l, 5], 2048)]
        vsv = [s_[:].rearrange("p (k n) -> p k n", k=8) for s_ in vs]
        vdone = set()

        def v_proj(tt_i):
            vdone.add(tt_i)
            pb = bank((0, 1, 2, 3, 4))
            for g in range(2):
                for kc in range(8):
                    mm(pb[:, g * 256:(g + 1) * 256], hTv[:, kc, tt_i * 128:(tt_i + 1) * 128], vsv[g][:, kc, :],
                       start=(kc == 0), stop=(kc == 7), skip=True)
            src = pb[:].rearrange("p (h d) -> p h d", h=8)
            cp(Vaug[:, tt_i, :, 0:64], src, eng="dve")

        def attn(c):
            qz = qkset[c % 2][0:2]
            kz = qkset[c % 2][2:4]
            tasks = [(qb, hh, j) for qb in range(4) for hh in range(2) for j in range(4 * qb + 4)]
            LA = 3
            info = {}

            def emit_score(idx):
                qb, hh, j = tasks[idx]
                h = 2 * c + hh
                if j not in vdone:
                    v_proj(j)
                r = max(0, j - 4 * qb)
                n = TB - 128 * r
                q0 = qb * TB + 128 * r
                sp_ = bank((0, 1, 2, 3, 4))
                diagt = (j >= 4 * qb)
                mm(sp_[:, 0:n], kz[hh][:, j * 128:(j + 1) * 128], qz[hh][:, q0:q0 + n], start=True, stop=not diagt)
                if diagt:
                    mm(sp_[:, 0:128], identb[:], negmask[:], start=False, stop=True)
                pt = PT[idx % 4]
                act(pt[:, 0:n], sp_[:, 0:n], AF.Exp, bias=negcumT[:, j * 8 + h:j * 8 + h + 1], scale=1.0)
                info[idx] = (pt, r)

            def emit_pv(idx):
                qb, hh, j = tasks[idx]
                h = 2 * c + hh
                bp = 64 * hh
                pt, r = info.pop(idx)
                acc = ps[5 + (qb * 2 + hh) % 2]
                if j == 0:
                    mm(acc[:, 0:260], zerob[:, 0:128], zerob[:, 0:260], start=True, stop=False, skip=True)
                accv = acc[:, 0:260].rearrange("p (i d) -> p i d", i=4)
                for ii in range(r, 4):
                    mm(accv[:, ii, :], pt[:, (ii - r) * 128:(ii - r + 1) * 128], Vaug[:, j, h, :], start=False,
                       stop=(j == 4 * qb + ii), skip=True)
                if j == 4 * qb + 3:
                    otok = otoks[qb % 2][:].rearrange("p (i d) -> p i d", i=4)
                    rc = rcs[:, hh * 4:4 + hh * 4]
                    recip(rc, accv[:, :, 64])
                    tt(otok[:, :, bp:bp + 64], accv[:, :, 0:64], rc.unsqueeze(2).to_broadcast([128, 4, 64]), ALU.mult)
                    if hh == 1:
                        pbt = ps[7]
                        pbtv = pbt[:].bitcast(BF16)[:, 0:512]
                        for ii in range(4):
                            tr(pbtv[:, ii * 128:(ii + 1) * 128], otok[:, ii, :], identb[:])
                        cp(oT[:, c, qb * TB:(qb + 1) * TB], pbtv, eng="dve")

            for idx in range(len(tasks) + LA):
                if idx < len(tasks):
                    emit_score(idx)
                if idx >= LA:
                    emit_pv(idx - LA)

        for c in range(4):
            if c + 1 < 4:
                qk_proj(c + 1)
            attn(c)
        if dbg and l == 0:
            dbg_ops.append(dma(dbg_d["oT"], arv(0, 16384)))

        vpad = arv(16384, 16640).rearrange("p (c t) -> p c t", c=4)
        yA = vpad[:, :, PADV:PADV + T]
        Cc = arv(33280, 32768, F32).rearrange("p (c t) -> p c t", c=4)
        diag = arv(66048, 7936).rearrange("p (k n) -> p k n", k=31)
        mset(vpad[:, :, 0:PADV], 0.0)
        def conv_ln(tb):
            p1 = bank()
            p2 = bank()
            for i in range(4):
                cb = tmp[4][:].bitcast(BF16)[:, 0:512]
                cq = tmp[4][:].bitcast(BF16)[:, 512:1024]
                cp(cb, Cc[:, i, tb * TB:(tb + 1) * TB], eng="dve")
                act(cq, Cc[:, i, tb * TB:(tb + 1) * TB], AF.Square)
                mm(p1[:], onesb[:], cb, start=(i == 0), stop=(i == 3))
                mm(p2[:], onesb[:], cq, start=(i == 0), stop=(i == 3))
            mean = tmp[0][:]
            var = tmp[1][:]
            act(mean, p1[:], AF.Identity, scale=1.0 / 512.0)
            tt(var, mean, mean, ALU.mult)
            stt(var, p2[:], 1.0 / 512.0, var, ALU.mult, ALU.subtract)
            act(var, var, AF.Ln, bias=eps_col, scale=1.0)
            act(var, var, AF.Exp, scale=-0.5)
            for i in range(4):
                t1 = tmp[2 + i % 2][:]
                tt(t1, Cc[:, i, tb * TB:(tb + 1) * TB], mean, ALU.subtract)
                tt(t1, t1, var, ALU.mult)
                act(yA[:, i, tb * TB:(tb + 1) * TB], t1, AF.Silu, bias=pvt[:, CLB + i:CLB + i + 1], scale=pvt[:, CLG + i:CLG + i + 1])

        for i in range(4):
            slot = wload(win_d[l, 6 + i], 2048)
            sv = slot[:].rearrange("p (k n) -> p k n", k=8)
            for tb in range(NTB):
                pg = bank()
                pa = bank()
                for kc in range(8):
                    mm(pg[:], sv[:, kc, 0:128], hTv[:, kc, tb * TB:(tb + 1) * TB], start=(kc == 0), stop=(kc == 7))
                for kc in range(8):
                    mm(pa[:], sv[:, kc, 128:256], hTv[:, kc, tb * TB:(tb + 1) * TB], start=(kc == 0), stop=(kc == 7))
                sg = tmp[tb % 2][:]
                act(sg, pg[:], AF.Sigmoid)
                tt(vpad[:, i, PADV + tb * TB:PADV + (tb + 1) * TB], pa[:], sg, ALU.mult)
            NDT = 6
            for k in range(NDT, 31):
                tsc(diag[:, k, :], identb[:], pvt[:, CW + i * 31 + k:CW + i * 31 + k + 1], None, ALU.mult)
            for tb in range(NTB):
                pc = bank()
                for k in range(NDT, 31):
                    mm(pc[:], diag[:, k, :], vpad[:, i, tb * TB + k + 2:tb * TB + k + 2 + TB], start=(k == NDT), stop=(k == 30))
                cblk = Cc[:, i, tb * TB:(tb + 1) * TB]
                act(cblk, pc[:], AF.Identity, bias=pvt[:, CB + i:CB + i + 1], scale=1.0)
                for k in range(NDT):
                    stt(cblk, vpad[:, i, tb * TB + k + 2:tb * TB + k + 2 + TB], pvt[:, CW + i * 31 + k:CW + i * 31 + k + 1], cblk, ALU.mult, ALU.add)
                if i == 3 and tb >= 1:
                    conv_ln(tb - 1)
        conv_ln(NTB - 1)
        if dbg and l == 0:
            for i in range(4):
                dbg_ops.append(dma(dbg_d["yA"][:, i * T:(i + 1) * T], yA[:, i, :]))

        yC = arv(33280, 16384).rearrange("p (c t) -> p c t", c=4)
        QT = 512
        Xbs = [[arv(49664, 2064, F32), arv(69120, 2064, F32)], [arv(52224, 2064, F32), arv(71680, 2064, F32)]]
        XRs = [[arv(54784, 2048, F32), tmp[2][:]], [arv(56832, 2048, F32), tmp[3][:]]]
        RAs = [arv(58880, 2048, F32), arv(60928, 2048, F32)]
        MHs = [arv(62976, 2048, F32), arv(65024, 2048, F32)]
        xrbs = [[arv(67072, 1024), tmp[4][:].bitcast(BF16)[:, 0:512]], [arv(68096, 1024), tmp[4][:].bitcast(BF16)[:, 512:1024]]]
        gys = [tmp[0][:], tmp[1][:]]
        hlasts = [lrs[:, 0:1], lrs[:, 1:2]]
        halos = [lrs[:, 2:5], lrs[:, 5:8]]
        lru_w = {}

        def lru_A(it):
            pr, q = it // 4, it % 4
            b = it % 2
            if q == 0:
                sl = [wload(win_d[l, 10 + 2 * pr + s_], 2048) for s_ in range(2)]
                bdslot = wload(bd_d[l][:, 512 * pr:512 * pr + 512], 512)
                lru_w[pr] = ([x_[:].rearrange("p (k n) -> p k n", k=8) for x_ in sl],
                             bdslot[:, 0:512].rearrange("p (i g n) -> p i g n", i=2, g=2))
            svs, bdv = lru_w[pr]
            for s_ in range(2):
                Xb = Xbs[s_][b]
                if q == 0:
                    mset(Xb[:, 0:3], 0.0, eng="dve")
                else:
                    cp(Xb[:, 0:3], halos[s_])
            for s_ in range(2):
                Xb = Xbs[s_][b]
                px = bank()
                for kc in range(8):
                    mm(px[:], svs[s_][:, kc, 0:128], hTv[:, kc, q * QT:(q + 1) * QT], start=(kc == 0), stop=(kc == 7))
                cp(Xb[:, 3:3 + QT], px[:], eng="act")
            for k in range(4):
                for s_ in range(2):
                    i = 2 * pr + s_
                    Xb, XR = Xbs[s_][b], XRs[s_][b]
                    if k == 0:
                        tsc(XR, Xb[:, 0:QT], pvt[:, LW + i * 4:LW + i * 4 + 1], pvt[:, LB + i:LB + i + 1], ALU.mult, ALU.add)
                    else:
                        stt(XR, Xb[:, k:k + QT], pvt[:, LW + i * 4 + k:LW + i * 4 + k + 1], XR, ALU.mult, ALU.add)
            for s_ in range(2):
                cp(xrbs[s_][b], XRs[s_][b], eng="dve")
                if q < 3:
                    cp(halos[s_], Xbs[s_][b][:, QT:QT + 3])

        def lru_B(it):
            pr, q = it // 4, it % 4
            b = it % 2
            svs, bdv = lru_w[pr]
            two = range(2)
            Ibs = [Xbs[s_][b][:, 4:4 + QT] for s_ in two]
            for s_ in two:
                py = bank()
                for kc in range(8):
                    mm(py[:], svs[s_][:, kc, 128:256], hTv[:, kc, q * QT:(q + 1) * QT], start=(kc == 0), stop=(kc == 7))
                act(gys[s_], py[:], AF.Gelu_apprx_tanh)
            prs = []
            for s_ in two:
                pr_ = bank()
                pi_ = bank()
                mm(pr_[:], bdv[:, s_, 0, :], xrbs[s_][b])
                mm(pi_[:], bdv[:, s_, 1, :], xrbs[s_][b])
                prs.append((pr_, pi_))
            for s_ in two:
                i = 2 * pr + s_
                act(RAs[s_], prs[s_][0][:], AF.Tanh, bias=dv[:, HBR + i:HBR + i + 1], scale=0.5)
            for s_ in two:
                i = 2 * pr + s_
                act(Ibs[s_], prs[s_][1][:], AF.Tanh, bias=dv[:, HBI + i:HBI + i + 1], scale=0.5)
            for s_ in two:
                i = 2 * pr + s_
                act(MHs[s_], RAs[s_], AF.Exp, bias=dv[:, CL + i:CL + i + 1], scale=dv[:, CL + i:CL + i + 1])
            for s_ in two:
                i = 2 * pr + s_
                act(RAs[s_], RAs[s_], AF.Exp, bias=dv[:, HCL + i:HCL + i + 1], scale=dv[:, HCL + i:HCL + i + 1])
            for s_ in two:
                tsc(MHs[s_], MHs[s_], 0.99999994, None, ALU.min)
            for s_ in two:
                act(MHs[s_], MHs[s_], AF.Sqrt, bias=small[:, 4:5], scale=-0.25)
            for s_ in two:
                stt(Ibs[s_], Ibs[s_], 1.0, XRs[s_][b], ALU.add, ALU.mult)
            for s_ in two:
                tt(Ibs[s_], Ibs[s_], MHs[s_], ALU.mult)
            for s_ in two:
                init = 0.0 if q == 0 else hlasts[s_]
                rd = [RAs[s_], Ibs[s_]] + ([hlasts[s_]] if q > 0 else [])
                S.add("dve", (lambda init_, Ib_, MH_, RA_: (lambda e: e.tensor_tensor_scan(out=MH_, data0=RA_, data1=Ib_, initial=init_,
                                                                                            op0=ALU.mult, op1=ALU.add)))(init, Ibs[s_], MHs[s_], RAs[s_]),
                      reads=rd, writes=[MHs[s_]])
            for s_ in two:
                if q < 3:
                    cp(hlasts[s_], MHs[s_][:, QT - 1:QT])
            for s_ in two:
                i = 2 * pr + s_
                tt(yC[:, i, q * QT:(q + 1) * QT], gys[s_], MHs[s_], ALU.mult)

        for pr in range(2):
            lru_A(4 * pr)
            for q in range(4):
                it = 4 * pr + q
                if q < 3:
                    lru_A(it + 1)
                lru_B(it)
        if dbg and l == 0:
            dbg_ops.append(dma(dbg_d["yC"], arv(33280, 16384)))

        mg = arv(49664, 16384).rearrange("p (c t) -> p c t", c=4)
        brs = [yA, oT, yC]
        for half in range(2):
            for mm_i in range(4):
                m = half * 4 + mm_i
                g2 = wload(wg2_d[l, m], 2048)
                g1 = wload(wg1_d[l, m], 1024)
                g2v = g2[:].rearrange("p (k n) -> p k n", k=8)
                g1v = g1[:, 0:1024].rearrange("p (k n) -> p k n", k=8)
                gw = [g2v[:, :, 0:128], g2v[:, :, 128:256], g1v[:, :, 0:128]]
                pslot = wload(pout_d[l, m], 1536)
                pv4 = pslot[:, 0:1536].rearrange("p (b k n) -> p b k n", b=3, k=4)
                for tb in range(NTB):
                    sgs = []
                    for b in range(3):
                        pg = bank()
                        for kc in range(8):
                            mm(pg[:], gw[b][:, kc, :], hTv[:, kc, tb * TB:(tb + 1) * TB], start=(kc == 0), stop=(kc == 7))
                        sgt = tmp[b][:]
                        act(sgt, pg[:], AF.Sigmoid)
                        sgs.append(sgt)
                    acc1 = tmp[3][:]
                    acc2 = tmp[4][:]
                    for b in range(3):
                        pbr = bank()
                        for kc in range(4):
                            mm(pbr[:], pv4[:, b, kc, :], brs[b][:, kc, tb * TB:(tb + 1) * TB], start=(kc == 0), stop=(kc == 3))
                        if b == 0:
                            tt(acc1, pbr[:], sgs[b], ALU.mult)
                        else:
                            tt(acc2, pbr[:], sgs[b], ALU.mult)
                            if b == 1:
                                tt(acc1, acc1, acc2, ALU.add)
                            else:
                                tt(mg[:, mm_i, tb * TB:(tb + 1) * TB], acc1, acc2, ALU.add)
            if half == 0:
                for g in range(4):
                    slot = wload(wo_d[l, g], 2048)
                    sv = slot[:].rearrange("p (k n) -> p k n", k=8)
                    for jj in range(2):
                        mo = 2 * g + jj
                        for tb in range(NTB):
                            po = bank()
                            for kc in range(4):
                                mm(po[:], sv[:, half * 4 + kc, jj * 128:(jj + 1) * 128], mg[:, kc, tb * TB:(tb + 1) * TB], start=(kc == 0), stop=(kc == 3))
                            xs = xTv[:, mo, tb * TB:(tb + 1) * TB]
                            stt(xs, po[:], dv[:, MOD + 16 + mo:MOD + 17 + mo], xs, ALU.mult, ALU.add)
            else:
                default_pool[0] = tuple(range(7))
                svs4 = [wload(wo_d[l, g], 2048)[:].rearrange("p (k n) -> p k n", k=8) for g in range(4)]
                for tb in range(NTB):
                    for mo in range(8):
                        po = bank()
                        for kc in range(4):
                            mm(po[:], svs4[mo // 2][:, half * 4 + kc, (mo % 2) * 128:(mo % 2) * 128 + 128], mg[:, kc, tb * TB:(tb + 1) * TB], start=(kc == 0), stop=(kc == 3))
                        xs = xTv[:, mo, tb * TB:(tb + 1) * TB]
                        stt(xs, po[:], dv[:, MOD + 16 + mo:MOD + 17 + mo], xs, ALU.mult, ALU.add)
                    norm_stats(tb)
                    if tb > 0:
                        norm_apply(tb - 1, dv, A2, MOD + 24)
                norm_apply(NTB - 1, dv, A2, MOD + 24)
        if dbg and l == 0:
            dbg_ops.append(dma(dbg_d["xmid"], xT[:]))

        default_pool[0] = tuple(range(7))
        nxt = (l + 1 < NL)
        ada_next = list(range(24)) if nxt else []
        pvt_n, dv_n = pvts[(l + 1) % 2], dvs[(l + 1) % 2]
        if nxt:
            dma(pvt_n[:], pv_d[l + 1])
        hid = arv(0, 45056).rearrange("p (j t) -> p j t", j=22)
        for hf in range(2):
            for j in range(22):
                slot = wload(wfi_d[l, j], 2048)
                sv = slot[:].rearrange("p (k n) -> p k n", k=8)
                for t2 in range(2):
                    tb = hf * 2 + t2
                    pa = bank()
                    pb = bank()
                    for kc in range(8):
                        mm(pa[:], sv[:, kc, 0:128], hTv[:, kc, tb * TB:(tb + 1) * TB], start=(kc == 0), stop=(kc == 7))
                    for kc in range(8):
                        mm(pb[:], sv[:, kc, 128:256], hTv[:, kc, tb * TB:(tb + 1) * TB], start=(kc == 0), stop=(kc == 7))
                    sa = tmp[(2 * j + t2) % 4][:]
                    act(sa, pa[:], AF.Silu)
                    tt(hid[:, j, t2 * TB:(t2 + 1) * TB], pb[:], sa, ALU.mult)
                if hf == 0 and ada_next:
                    ada_group(l + 1, ada_next.pop(0))
                if hf == 1 and nxt:
                    if j == 1:
                        norm_stats(0, scr_ffn)
                    elif j == 3:
                        norm_stats(1, scr_ffn)
                    elif j == 5:
                        norm_apply(0, dv_n, A1, MOD + 0, scr_ffn)
                    elif j == 8:
                        norm_apply(1, dv_n, A1, MOD + 0, scr_ffn)
            if hf == 0:
                for mo in range(8):
                    sl0 = wload(wfo_d[l, 2 * mo], 1408)
                    sl1 = wload(wfo_d[l, 2 * mo + 1], 1408)
                    svs = [s_[:, 0:1408].rearrange("p (k n) -> p k n", k=11) for s_ in (sl0, sl1)]
                    for t2 in range(2):
                        tb = t2
                        po = bank()
                        for j in range(22):
                            mm(po[:], svs[j // 11][:, j % 11, :], hid[:, j, t2 * TB:(t2 + 1) * TB], start=(j == 0), stop=(j == 21))
                        xs = xTv[:, mo, tb * TB:(tb + 1) * TB]
                        stt(xs, po[:], dv[:, MOD + 40 + mo:MOD + 41 + mo], xs, ALU.mult, ALU.add)
                    if ada_next:
                        ada_group(l + 1, ada_next.pop(0))
                if nxt:
                    assert not ada_next
                    finalize_params(pvt_n, dv_n)
            else:
                for t2 in range(2):
                    tb = 2 + t2
                    for mo in range(8):
                        sl0 = wload(wfo_d[l, 2 * mo], 1408)
                        sl1 = wload(wfo_d[l, 2 * mo + 1], 1408)
                        svs = [s_[:, 0:1408].rearrange("p (k n) -> p k n", k=11) for s_ in (sl0, sl1)]
                        po = bank()
                        for j in range(22):
                            mm(po[:], svs[j // 11][:, j % 11, :], hid[:, j, t2 * TB:(t2 + 1) * TB], start=(j == 0), stop=(j == 21))
                        xs = xTv[:, mo, tb * TB:(tb + 1) * TB]
                        stt(xs, po[:], dv[:, MOD + 40 + mo:MOD + 41 + mo], xs, ALU.mult, ALU.add)
                        if nxt and t2 == 1 and mo == 1:
                            norm_stats(2, scr_ffn)
                        if nxt and t2 == 1 and mo == 4:
                            norm_apply(2, dv_n, A1, MOD + 0, scr_ffn)
                if nxt:
                    norm_stats(3, scr_ffn)
                    norm_apply(3, dv_n, A1, MOD + 0, scr_ffn)

    outs = []
    for tt_i in range(16):
        xio = xios[tt_i % 4]
        for half in range(2):
            pb = bank()
            for q in range(4):
                kc = half * 4 + q
                tr(pb[:, q * 128:(q + 1) * 128], xTv[:, kc, tt_i * 128:(tt_i + 1) * 128], identf[:])
            if half == 0:
                cp(xio[:, 0:512], pb[:], eng="dve")
            else:
                cp(xio[:, 512:1024], pb[:], eng="act")
        outs.append(dma(out_d[tt_i * 128:(tt_i + 1) * 128, :], xio))
    S.emit(final_wait_ops=outs + dbg_ops)
    es.close()
    return nc


def _win_perm():
    order = []
    for c in range(4):
        order += [1024 + 128 * c, 1536 + 128 * c]
    for c in range(4):
        order += [2048 + 128 * c]
    for i in range(4):
        order += [512 + 128 * i, 128 * i]
    for i in range(4):
        order += [3080 + 128 * i, 2568 + 128 * i]
    for m in range(8):
        order += [3592 + 128 * m, 4616 + 128 * m, 5640 + 128 * m]
    cols = np.concatenate([np.arange(s, s + 128) for s in order])
    return cols


def _kmajor(w, ncols_per_group):
    L, K, C = w.shape
    kc = K // 128
    g = ncols_per_group
    a = w.reshape(L, kc, 128, C // g, g)
    a = a.transpose(0, 3, 2, 1, 4)
    return np.ascontiguousarray(a.reshape(L, C // g, 128, kc * g))


def _chunkvec(v):
    L, n = v.shape
    return v.reshape(L, n // 128, 128).transpose(0, 2, 1)


def prep_inputs(inp, NL=NLAYERS):
    f = lambda a: np.asarray(a, dtype=np.float32)[:NL]
    shared = {}
    shared["wada"] = _kmajor(f(inp["w_ada"]), 256)
    perm = _win_perm()
    win = f(inp["w_in"])
    shared["win"] = _kmajor(np.ascontiguousarray(win[:, :, perm[:28 * 128]]), 256)
    g2cols = np.concatenate([np.concatenate([np.arange(3592 + 128 * m, 3592 + 128 * m + 128), np.arange(4616 + 128 * m, 4616 + 128 * m + 128)]) for m in range(8)])
    g1cols = np.concatenate([np.arange(5640 + 128 * m, 5640 + 128 * m + 128) for m in range(8)])
    shared["wg2"] = _kmajor(np.ascontiguousarray(win[:, :, g2cols]), 256)
    shared["wg1"] = _kmajor(np.ascontiguousarray(win[:, :, g1cols]), 128)
    wfc = win[:, :, 2560:2568]
    wf = np.concatenate([wfc, wfc], axis=2)
    shared["wf"] = np.ascontiguousarray(wf.reshape(NL, 8, 128, 16).transpose(0, 2, 1, 3).reshape(NL, 128, 128))
    bd = np.zeros((NL, 128, 8, 128), np.float32)
    wr = f(inp["lru_w_r"])
    wi = f(inp["lru_w_i"])
    for i in range(4):
        for g, w in enumerate((wr, wi)):
            bd[:, 0:64, 2 * i + g, 0:64] = w[:, 2 * i]
            bd[:, 64:128, 2 * i + g, 64:128] = w[:, 2 * i + 1]
    shared["bd"] = bd.reshape(NL, 128, 1024)
    pA = _kmajor(f(inp["w_conv_out"]), 128)
    pB = _kmajor(f(inp["w_fox_out"]), 128)
    pC = _kmajor(f(inp["w_lru_out"]), 128)
    shared["pout"] = np.ascontiguousarray(np.concatenate([pA, pB, pC], axis=3))
    shared["wo"] = _kmajor(f(inp["w_o"]), 256)
    wfi = f(inp["w_ffn_in"])
    fperm = np.concatenate([np.concatenate([np.arange(128 * j, 128 * j + 128), np.arange(2816 + 128 * j, 2816 + 128 * j + 128)]) for j in range(22)])
    shared["wfi"] = _kmajor(np.ascontiguousarray(wfi[:, :, fperm]), 256)
    wfo = f(inp["w_ffn_out"])
    a = wfo.reshape(NL, 2, 11, 128, 8, 128)
    a = a.transpose(0, 4, 1, 3, 2, 5)
    shared["wfo"] = np.ascontiguousarray(a.reshape(NL, 16, 128, 1408))
    pv = np.zeros((NL, 128, NPV), np.float32)
    pv[:, :, BADA:BADA + 48] = _chunkvec(f(inp["b_ada"]))
    pv[:, :, GN1:GN1 + 8] = _chunkvec(f(inp["g_norm_mix"]))
    pv[:, :, GN2:GN2 + 8] = _chunkvec(f(inp["g_norm_ffn"]))
    cw = f(inp["conv_dw_w"])
    pv[:, :, CW:CW + 124] = cw.reshape(NL, 31, 4, 128).transpose(0, 3, 2, 1).reshape(NL, 128, 124)
    pv[:, :, CB:CB + 4] = _chunkvec(f(inp["conv_dw_b"]))
    pv[:, :, CLG:CLG + 4] = _chunkvec(f(inp["conv_ln_g"]))
    pv[:, :, CLB:CLB + 4] = _chunkvec(f(inp["conv_ln_b"]))
    pv[:, :, GQ] = np.tile(f(inp["fox_q_norm_g"]), (1, 2))
    pv[:, :, GK] = np.tile(f(inp["fox_k_norm_g"]), (1, 2))
    pv[:, :, BF_] = np.tile(f(inp["fox_b_f"]), (1, 16))
    lw = f(inp["lru_conv_w"])
    pv[:, :, LW:LW + 16] = lw.reshape(NL, 4, 4, 128).transpose(0, 3, 2, 1).reshape(NL, 128, 16)
    pv[:, :, LB:LB + 4] = _chunkvec(f(inp["lru_conv_b"]))
    pv[:, :, LBR:LBR + 4] = _chunkvec(f(inp["lru_b_r"]))
    pv[:, :, LBI:LBI + 4] = _chunkvec(f(inp["lru_b_i"]))
    pv[:, :, LAM:LAM + 4] = _chunkvec(f(inp["lru_lambda"]))
    shared["pv"] = pv
    return shared


_NC_CACHE = {}


def kernel(**inputs):
    x = np.asarray(inputs["x"], dtype=np.float32)
    c = np.asarray(inputs["c"], dtype=np.float32)
    B = x.shape[0]
    shared = prep_inputs(inputs, NLAYERS)
    if "nc" not in _NC_CACHE:
        _NC_CACHE["nc"] = build(NLAYERS)
    nc = _NC_CACHE["nc"]
    in_maps = []
    for b in range(B):
        m = dict(shared)
        m["x"] = np.ascontiguousarray(x[b])
        m["c"] = np.ascontiguousarray(c[b].reshape(8, 128).T)
        in_maps.append(m)
    res = run_bass_kernel_spmd(nc, in_maps, core_ids=list(range(B)))
    return np.stack([np.asarray(r["out"], dtype=np.float32) for r in res.results], axis=0)
```
